# Optimizing a Trainium2 kernel written in Bass

```python
import jax, jax.numpy as jnp
from jax import lax
import numpy as np

D_MODEL = 1024
BATCH = 16
SEQ = 2048
DEPTH = 1

CHUNK = 64
Q_BLOCK = 128
HEAD_DIM = 64
N_HEADS_SB = 8
N_HEADS_FOX = 8
D_SB = N_HEADS_SB * HEAD_DIM
D_FOX = N_HEADS_FOX * HEAD_DIM
N_BRANCH = 2
IN_SIZES = (D_SB, D_SB, D_SB, D_SB,
            D_FOX, D_FOX, D_FOX, D_FOX,
            N_HEADS_FOX,
            N_BRANCH * D_MODEL)
D_IN = sum(IN_SIZES)
EPS = 1e-6
NEG_BIG = -1e30

kernel_name = "hybrid_stickbreaking_fox_gated_block"


def _rmsnorm(x, g):
    xf = x.astype(jnp.float32)
    inv = lax.rsqrt(jnp.mean(xf * xf, axis=-1, keepdims=True) + EPS)
    return (xf * inv).astype(x.dtype) * g


def _to_heads(t, n_heads):
    b, s, _ = t.shape
    return t.reshape(b, s, n_heads, HEAD_DIM).transpose(0, 2, 1, 3)


def _from_heads(t):
    b, h, s, d = t.shape
    return t.transpose(0, 2, 1, 3).reshape(b, s, h * d)


def _stick_breaking(q, k, v):
    s_len = q.shape[2]
    scale = HEAD_DIM ** -0.5
    outs = []
    for start in range(0, s_len, Q_BLOCK):
        end = start + Q_BLOCK
        qb, kb, vb = q[:, :, start:end], k[:, :, :end], v[:, :, :end]
        z = jnp.einsum('bhtd,bhsd->bhts', qb, kb).astype(jnp.float32) * scale
        t_idx = start + jnp.arange(Q_BLOCK)[:, None]
        s_idx = jnp.arange(end)[None, :]
        mask = s_idx < t_idx
        log_1m = jnp.where(mask, jax.nn.log_sigmoid(-z), 0.0)
        after = lax.cumsum(log_1m, axis=3, reverse=True) - log_1m
        w = jnp.where(mask, jnp.exp(jax.nn.log_sigmoid(z) + after), 0.0)
        outs.append(jnp.einsum('bhts,bhsd->bhtd', w.astype(vb.dtype), vb))
    return jnp.concatenate(outs, axis=2)


def _forgetting_attention(q, k, v, cum_log_f):
    s_len = q.shape[2]
    scale = HEAD_DIM ** -0.5
    outs = []
    for start in range(0, s_len, Q_BLOCK):
        end = start + Q_BLOCK
        qb, kb, vb = q[:, :, start:end], k[:, :, :end], v[:, :, :end]
        z = jnp.einsum('bhtd,bhsd->bhts', qb, kb).astype(jnp.float32) * scale
        z = z + cum_log_f[:, :, start:end, None] - cum_log_f[:, :, None, :end]
        t_idx = start + jnp.arange(Q_BLOCK)[:, None]
        s_idx = jnp.arange(end)[None, :]
        z = jnp.where(s_idx <= t_idx, z, NEG_BIG)
        p = jax.nn.softmax(z, axis=-1)
        outs.append(jnp.einsum('bhts,bhsd->bhtd', p.astype(vb.dtype), vb))
    return jnp.concatenate(outs, axis=2)


def setup_inputs(seed: int = 0) -> dict:
    key = jax.random.key(seed)
    ks = jax.random.split(key, 12)
    f32 = jnp.float32
    d = D_MODEL
    x = jax.random.normal(ks[0], (BATCH, SEQ, d), f32)
    c = jax.random.normal(ks[1], (BATCH, d), f32)
    w_ada = jax.random.normal(ks[2], (DEPTH, d, 3 * d), f32) * (0.5 * d ** -0.5)
    b_ada = 0.02 * jax.random.normal(ks[3], (DEPTH, 3 * d), f32)
    norm_g = 1.0 + 0.02 * jax.random.normal(ks[4], (DEPTH, d), f32)
    w_in = jax.random.normal(ks[5], (DEPTH, d, D_IN), f32) * d ** -0.5
    b_forget = jax.random.uniform(ks[6], (DEPTH, N_HEADS_FOX), f32, 1.0, 4.0)
    w_o_sb = jax.random.normal(ks[7], (DEPTH, D_SB, d), f32) * D_SB ** -0.5
    w_o_fox = jax.random.normal(ks[8], (DEPTH, D_FOX, d), f32) * D_FOX ** -0.5
    b_gate = 0.02 * jax.random.normal(ks[9], (DEPTH, N_BRANCH * d), f32)
    w_out = jax.random.normal(ks[10], (DEPTH, d, d), f32) * d ** -0.5
    final_g = 1.0 + 0.02 * jax.random.normal(ks[11], (d,), f32)
    return {"x": x, "c": c, "w_ada": w_ada, "b_ada": b_ada, "norm_g": norm_g,
            "w_in": w_in, "b_forget": b_forget, "w_o_sb": w_o_sb, "w_o_fox": w_o_fox,
            "b_gate": b_gate, "w_out": w_out, "final_g": final_g}


def reference(x, c, w_ada, b_ada, norm_g, w_in, b_forget, w_o_sb, w_o_fox, b_gate, w_out, final_g):
    b, s, d = x.shape
    split_points = list(np.cumsum(IN_SIZES)[:-1])
    for l in range(DEPTH):
        mod = c @ w_ada[l] + b_ada[l]
        shift, scale, gate = jnp.split(mod, 3, axis=-1)
        h = _rmsnorm(x, norm_g[l]) * (1.0 + scale[:, None, :]) + shift[:, None, :]

        proj = h @ w_in[l]
        (q_a, k_a, v_a, z_a, q_b, k_b, v_b, z_b,
         f_logit, g_logit) = jnp.split(proj, split_points, axis=-1)

        y_a = _from_heads(_stick_breaking(_to_heads(q_a, N_HEADS_SB), _to_heads(k_a, N_HEADS_SB),
                                          _to_heads(v_a, N_HEADS_SB)))
        y_a = (y_a * jax.nn.silu(z_a)) @ w_o_sb[l]

        log_f = jax.nn.log_sigmoid((f_logit + b_forget[l]).astype(jnp.float32))
        cum_log_f = jnp.cumsum(log_f, axis=1).transpose(0, 2, 1)
        y_b = _from_heads(_forgetting_attention(_to_heads(q_b, N_HEADS_FOX), _to_heads(k_b, N_HEADS_FOX),
                                                _to_heads(v_b, N_HEADS_FOX), cum_log_f))
        y_b = (y_b * jax.nn.silu(z_b)) @ w_o_fox[l]

        gates = jax.nn.sigmoid(g_logit + b_gate[l]).reshape(b, s, N_BRANCH, d)
        merged = gates[:, :, 0, :] * y_a + gates[:, :, 1, :] * y_b
        x = x + gate[:, None, :] * (merged @ w_out[l])
    return _rmsnorm(x, final_g)
```

```python
import numpy as np
import concourse.bass as bass
import concourse.mybir as mybir
from concourse.bass_utils import run_bass_kernel_spmd

F32 = mybir.dt.float32
BF16 = mybir.dt.bfloat16
ALU = mybir.AluOpType
AF = mybir.ActivationFunctionType

NCORES = 8
S = 2048
D = 1024
NSEQ = 2
KC = 8
NB = S // 128
NTG = S // 512
EPS = 1e-6
OFF_QA, OFF_KA, OFF_VA, OFF_ZA = 0, 512, 1024, 1536
OFF_QB, OFF_KB, OFF_VB, OFF_ZB = 2048, 2560, 3072, 3584
OFF_F, OFF_G = 4096, 4104
D_IN = 6152
MASKVAL = -30000.0

ENGS = ["sp", "act", "dve", "pool", "pe"]


class Op:
    __slots__ = ("eng", "fn", "deps", "signal", "sigval", "dma_sem", "dma_val", "idx")

    def __init__(self, eng, fn):
        self.eng = eng
        self.fn = fn
        self.deps = []
        self.signal = False
        self.sigval = 0
        self.dma_sem = None
        self.dma_val = 0


class Prog:
    def __init__(self, nc):
        self.nc = nc
        self.ops = {e: [] for e in ENGS}
        self.last_writer = {}
        self.readers = {}
        self.dma_counts = {}
        self.all_dma_ops = []

    def op(self, eng, fn, reads=(), writes=(), dma_key=None):
        o = Op(eng, fn)
        deps = {}
        for t in reads:
            w = self.last_writer.get(t)
            if w is not None:
                deps[id(w)] = w
        for t in writes:
            w = self.last_writer.get(t)
            if w is not None:
                deps[id(w)] = w
            for r in self.readers.get(t, ()):
                deps[id(r)] = r
        for d in deps.values():
            if d is o:
                continue
            if d.dma_sem is None and d.eng == eng and eng in ("pe", "sp"):
                continue
            o.deps.append(d)
        for t in reads:
            self.readers.setdefault(t, []).append(o)
        for t in writes:
            self.last_writer[t] = o
            self.readers[t] = []
        if dma_key is not None:
            c = self.dma_counts.get(dma_key, 0) + 1
            self.dma_counts[dma_key] = c
            o.dma_sem = dma_key
            o.dma_val = 16 * c
            self.all_dma_ops.append(o)
        self.ops[eng].append(o)
        return o

    def barrier(self):
        lasts = []
        for e in ENGS:
            real = [o for o in self.ops[e] if o.fn is not None]
            if real:
                lasts.append(real[-1])
        dma_last = {}
        for o in self.all_dma_ops:
            dma_last[o.dma_sem] = o
        for e in ENGS:
            o = Op(e, None)
            for l in lasts:
                if l.eng != e or l.dma_sem is not None:
                    o.deps.append(l)
            for d in dma_last.values():
                o.deps.append(d)
            self.ops[e].append(o)

    def emit(self):
        nc = self.nc
        for e in ENGS:
            for o in self.ops[e]:
                for d in o.deps:
                    if d.dma_sem is None:
                        d.signal = True
        sems = {e: nc.alloc_semaphore("done_" + e) for e in ENGS}
        dsems = {k: nc.alloc_semaphore("dma_%s" % (str(k).replace(" ", "").replace("'", "").replace("(", "_").replace(")", "_").replace(",", "_")))
                 for k in self.dma_counts}
        for e in ENGS:
            c = 0
            for o in self.ops[e]:
                if o.signal:
                    c += 1
                    o.sigval = c
        prog = self

        def run(e, eng):
            waited = {}
            for o in prog.ops[e]:
                need = {}
                for d in o.deps:
                    if d.dma_sem is not None:
                        key = ("d", d.dma_sem)
                        sem = dsems[d.dma_sem]
                        val = d.dma_val
                    else:
                        key = ("e", d.eng)
                        sem = sems[d.eng]
                        val = d.sigval
                    if val > waited.get(key, 0) and val > need.get(key, (None, 0))[1]:
                        need[key] = (sem, val)
                for key, (sem, val) in need.items():
                    eng.wait_ge(sem, val)
                    waited[key] = val
                if o.fn is None:
                    continue
                inst = o.fn(eng)
                if o.dma_sem is not None:
                    inst.then_inc(dsems[o.dma_sem], 16)
                elif o.signal:
                    inst.then_inc(sems[e], 1)

        with nc.Block() as block:
            @block.sync
            def _(eng):
                run("sp", eng)

            @block.scalar
            def _(eng):
                run("act", eng)

            @block.vector
            def _(eng):
                run("dve", eng)

            @block.gpsimd
            def _(eng):
                run("pool", eng)

            @block.tensor
            def _(eng):
                run("pe", eng)


def build_nc():
    nc = bass.Bass("TRN2", target_bir_lowering=False)
    P = Prog(nc)

    def dram_in(name, shape):
        return nc.dram_tensor(name, list(shape), F32, kind="ExternalInput").ap()

    x_d = dram_in("x", [NSEQ * S, D])
    cT_d = dram_in("cT", [128, KC * NSEQ])
    wada_d = dram_in("w_ada", [D, 3 * D])
    badaT_d = dram_in("badaT", [128, 24])
    ngT_d = dram_in("ngT", [128, KC])
    win_d = dram_in("w_in", [D, D_IN])
    bf_d = dram_in("bf", [8, 1])
    wosb_d = dram_in("w_o_sb", [512, D])
    wofox_d = dram_in("w_o_fox", [512, D])
    bgT_d = dram_in("bgT", [128, 16])
    wout_d = dram_in("w_out", [D, D])
    fgT_d = dram_in("fgT", [128, KC])
    ident_d = dram_in("identf", [128, 128])
    maskA_d = dram_in("maskA", [128, 128])
    maskB_d = dram_in("maskB", [128, 128])
    out_d = nc.dram_tensor("out", [NSEQ * S, D], F32, kind="ExternalOutput").ap()

    base = int(nc.sbuf_base)
    base = (base + 63) // 64 * 64
    top = int(nc.sbuf_top)
    cur = [base]
    offs = {}

    def sb(name, shape, dtype, at=None):
        nbytes = int(np.prod(shape[1:])) * (4 if dtype == F32 else 2)
        nbytes = (nbytes + 63) // 64 * 64
        if at is None:
            off = cur[0]
            cur[0] += nbytes
        else:
            off = at
        assert off + nbytes <= top, (name, off, nbytes, top)
        offs[name] = off
        return nc.alloc_sbuf_tensor_at(name, list(shape), dtype, offset=off), off + nbytes

    def sbp(name, shape, dtype):
        return sb(name, shape, dtype)[0]

    identf = sbp("identf", [128, 128], F32)
    onesf = sbp("onesf", [128, 128], F32)
    ident = sbp("ident", [128, 128], BF16)
    maskA = sbp("maskA", [128, 128], BF16)
    maskB = sbp("maskB", [128, 128], BF16)
    mstage = sbp("mstage", [128, 128], F32)
    cT = sbp("cT", [128, KC, NSEQ], F32)
    cTb = sbp("cTb", [128, KC, NSEQ], BF16)
    badaT = sbp("badaT", [128, 24], F32)
    ngT = sbp("ngT", [128, KC], F32)
    bgT = sbp("bgT", [128, 16], F32)
    fgT = sbp("fgT", [128, KC], F32)
    bfc = sbp("bfc", [8, 1], F32)
    modT = sbp("modT", [128, 24, NSEQ], F32)
    acoef = sbp("acoef", [128, KC, NSEQ], F32)
    eps_t = sbp("eps_t", [128, 1], F32)
    ones_c = sbp("ones_c", [128, 1], F32)
    negb = sbp("negb", [8, 1], F32)
    stat = sbp("stat", [128, 8], F32)
    diag = [sbp("diag%d" % i, [128, 128], F32) for i in range(2)]
    gate_bc = sbp("gate_bc", [128, D], F32)
    fg_bc = sbp("fg_bc", [128, D], F32)
    wosb = sbp("wosb", [128, 4, D], BF16)
    wofox = sbp("wofox", [128, 4, D], BF16)
    wout = sbp("wout", [128, KC, D], BF16)
    hT = sbp("hT", [128, KC, S], BF16)
    ygA = sbp("ygA", [128, 4, S], BF16)
    ygB = sbp("ygB", [128, 4, S], BF16)
    NSTG = 2
    stg = [sbp("stg%d" % i, [128, KC, 128], F32) for i in range(NSTG)]
    xt = [sbp("xt%d" % i, [128, D], F32) for i in range(2)]
    arena0 = cur[0]
    xn = [sbp("xn%d" % i, [128, D], BF16) for i in range(2)]
    wq = sbp("wq", [128, KC, 128], BF16)
    wk = sbp("wk", [128, KC, 128], BF16)
    wv = sbp("wv", [128, KC, 128], BF16)
    wz = sbp("wz", [128, KC, 128], BF16)
    wfw = sbp("wfw", [128, KC, 8], BF16)
    fsp = sbp("fsp", [128, S], BF16)
    QB = sbp("QB", [128, 2, S], BF16)
    KB = sbp("KB", [128, 2, S], BF16)
    zT = sbp("zT", [128, S], BF16)
    VV = sbp("VV", [128, NB, 2, 128], BF16)
    partA_end = cur[0]
    partB0 = cur[0]
    zT2 = sbp("zT2", [128, S], BF16)
    zTa = [zT, zT2]
    gbuf = [sbp("gbuf%d" % i, [128, 1024], F32) for i in range(2)]
    bbuf = [sbp("bbuf%d" % i, [128, 1024], BF16) for i in range(3)]
    Pbuf = [sbp("Pbuf%d" % i, [128, 1026], BF16) for i in range(4)]
    carry = sbp("carry", [128, 1], F32)
    wbuf = [sbp("wbuf%d" % i, [128, 1024], BF16) for i in range(3)]
    wTb = [sbp("wTb%d" % i, [128, 1024], BF16) for i in range(3)]
    zsig = [sbp("zsig%d" % i, [128, 512], BF16) for i in range(1)] * 2
    attn_end = cur[0]
    xt4 = xt + [nc.alloc_sbuf_tensor_at("xtx%d" % i, [128, D], F32, offset=offs["ygA"] + 4096 * i) for i in range(2)]
    stgx = [nc.alloc_sbuf_tensor_at("stgx%d" % i, [128, KC, 128], F32, offset=offs["hT"] + 4096 * i) for i in range(8)]
    YG_TOKS = [("ygA", tg) for tg in range(NTG)]
    VVa = [nc.alloc_sbuf_tensor_at("VVa%d" % i, [128, NB, 2, 64], BF16, offset=offs["xt%d" % i]) for i in range(2)]
    fmid = nc.alloc_sbuf_tensor_at("fmid", [8, 512], BF16, offset=offs["wTb0"])
    ftmp = [nc.alloc_sbuf_tensor_at("ftmp%d" % i, [8, 512], F32, offset=o_)
            for i, o_ in enumerate([offs["gbuf1"], offs["gbuf1"] + 2048, offs["wbuf0"], offs["wbuf1"]])]
    FT = [("g", 1), ("w", 0), ("w", 1), ("wT", 0)]
    pbuf = [nc.alloc_sbuf_tensor_at("pbuf%d" % i, [128, 512], BF16, offset=offs["xt0"] + 1024 * i) for i in range(4)]
    recb = nc.alloc_sbuf_tensor_at("recb", [128, 512], F32, offset=offs["xt1"])
    tmpb = nc.alloc_sbuf_tensor_at("tmpb", [128, 512], F32, offset=offs["xt1"] + 2048)
    wgp = [nc.alloc_sbuf_tensor_at("wgp%d" % i, [128, KC, 128], BF16, offset=partB0 + 2048 * i) for i in range(16)]
    wg_end = partB0 + 2048 * 16
    PARTB_TOKS = ([("za", 1, tg) for tg in range(NTG)] + [("g", 0), ("g", 1), ("b", 0), ("b", 1), ("b", 2), ("P", 0), ("P", 1), ("P", 2), ("P", 3),
                  ("Pc", 2), ("Pc2", 2), ("w", 0), ("w", 1), ("w", 2), ("wT", 0), ("wT", 1), ("wT", 2), "Pone", ("zsig", 0), ("zsig", 1)])
    cur[0] = arena0
    G0s = [sbp("G0s%d" % i, [128, 512], BF16) for i in range(2)]
    G1s = [sbp("G1s%d" % i, [128, 512], BF16) for i in range(2)]
    t0b = [sbp("t0b%d" % i, [128, 512], F32) for i in range(2)]
    t1b = [sbp("t1b%d" % i, [128, 512], F32) for i in range(2)]
    mT = [sbp("mT%d" % i, [128, KC, 512], BF16) for i in range(2)]
    xr = [sbp("xr%d" % i, [128, D], F32) for i in range(2)]
    obuf = [sbp("obuf%d" % i, [128, D], F32) for i in range(2)]
    s5_end = cur[0]
    assert s5_end <= partA_end, (s5_end, partA_end)
    assert max(attn_end, wg_end) <= top, (attn_end, wg_end, top)
    print("SBUF map: arena0=%d partA_end=%d attn_end=%d wg_end=%d s5_end=%d top=%d" % (arena0, partA_end, attn_end, wg_end, s5_end, top))

    TB = [nc.alloc_psum_tensor("tb%d" % i, [128, 1024], BF16) for i in range(2)]
    ZP = [nc.alloc_psum_tensor("zp%d" % i, [128, 1024], F32) for i in range(2)]
    PB45 = [nc.alloc_psum_tensor("pb%d" % i, [128, 512], F32) for i in (4, 5)]
    PB = [ZP[0][:, 0:512], ZP[0][:, 512:1024], ZP[1][:, 0:512], ZP[1][:, 512:1024], PB45[0], PB45[1]]
    TBF = [TB[0][:, :].bitcast(F32), TB[1][:, :].bitcast(F32)]
    sqs = TB[1]
    sqs5 = TB[1]

    def dma_in(key, out_ap, in_ap, reads=(), writes=(), eng="sp"):
        return P.op(eng, lambda e: e.dma_start(out=out_ap, in_=in_ap), reads=reads, writes=writes, dma_key=key)

    def load_const(dst, src, name):
        dma_in("c_" + name, dst[:], src, writes=[name])

    load_const(identf, ident_d[:, :], "identf")
    load_const(cT, cT_d.rearrange("p (k b) -> p k b", b=NSEQ), "cT")
    load_const(badaT, badaT_d[:, :], "badaT")
    load_const(ngT, ngT_d[:, :], "ngT")
    load_const(bgT, bgT_d[:, :], "bgT")
    load_const(fgT, fgT_d[:, :], "fgT")
    load_const(bfc, bf_d[:, :], "bfc")
    P.op("pool", lambda e: e.tensor_copy(out=ident[:], in_=identf[:]), reads=["identf"], writes=["ident"])
    dma_in("c_mask", mstage[:], maskA_d[:, :], writes=["mstage"])
    P.op("pool", lambda e: e.tensor_copy(out=maskA[:], in_=mstage[:]), reads=["mstage"], writes=["maskA"])
    dma_in("c_mask", mstage[:], maskB_d[:, :], writes=["mstage"])
    P.op("pool", lambda e: e.tensor_copy(out=maskB[:], in_=mstage[:]), reads=["mstage"], writes=["maskB"])
    P.op("pool", lambda e: e.memset(onesf[:], 1.0), writes=["onesf"])
    P.op("pool", lambda e: e.memset(eps_t[:], EPS), writes=["eps"])
    P.op("pool", lambda e: e.memset(ones_c[:], 1.0), writes=["ones_c"])
    P.op("dve", lambda e: e.tensor_scalar(out=negb[:], in0=bfc[:], scalar1=-1.0, scalar2=None, op0=ALU.mult), reads=["bfc"], writes=["negb"])
    P.op("pool", lambda e: e.memset(Pbuf[0][:, 1025:1026], 1.0), writes=["Pone"])
    P.op("pool", lambda e: e.memset(Pbuf[1][:, 1025:1026], 1.0), writes=["Pone"])
    P.op("pool", lambda e: e.memset(Pbuf[2][:, 1025:1026], 1.0), writes=["Pone"])

    stg_i = [0]

    def load_piece(src_ap, kcn, ncols, dst_ap, dst_tokens, cast_eng="pool"):
        i = stg_i[0] % NSTG
        stg_i[0] += 1
        st = stg[i]
        dma_in("stg%d" % i, st[:, 0:kcn, 0:ncols], src_ap, writes=[("stg", i)])
        P.op(cast_eng, lambda e: e.tensor_copy(out=dst_ap, in_=st[:, 0:kcn, 0:ncols]),
             reads=[("stg", i)], writes=dst_tokens)

    def staged_items(pieces, cast_eng):
        slots = {}

        def dma_fn(i):
            src_ap, kcn, ncols, dst_ap, toks = pieces[i]
            k = stg_i[0] % NSTG
            stg_i[0] += 1
            slots[i] = k
            dma_in("stg%d" % k, stg[k][:, 0:kcn, 0:ncols], src_ap, writes=[("stg", k)])

        def cast_fn(i):
            src_ap, kcn, ncols, dst_ap, toks = pieces[i]
            k = slots[i]
            st = stg[k]
            if cast_eng == "act":
                P.op("act", lambda e: e.activation(out=dst_ap, in_=st[:, 0:kcn, 0:ncols], func=AF.Copy),
                     reads=[("stg", k)], writes=toks)
            else:
                P.op(cast_eng, lambda e: e.tensor_copy(out=dst_ap, in_=st[:, 0:kcn, 0:ncols]),
                     reads=[("stg", k)], writes=toks)
        n = len(pieces)
        items = [lambda: [dma_fn(i) for i in range(min(NSTG, n))]]
        for i in range(n):
            def it(i=i):
                cast_fn(i)
                if i + NSTG < n:
                    dma_fn(i + NSTG)
            items.append(it)
        return items

    def chunk_pieces(offq, offk, offv, offz):
        return [(win_v[:, :, off:off + 128], KC, 128, dst[:, :, :], [nm])
                for nm, dst, off in (("wq", wq, offq), ("wk", wk, offk), ("wv", wv, offv), ("wz", wz, offz))]

    win_v = win_d.rearrange("(k p) n -> p k n", p=128)
    wada_v = wada_d.rearrange("(k p) n -> p k n", p=128)
    wosb_v = wosb_d.rearrange("(k p) n -> p k n", p=128)
    wofox_v = wofox_d.rearrange("(k p) n -> p k n", p=128)
    wout_v = wout_d.rearrange("(k p) n -> p k n", p=128)

    def bcast_rows(col_fn, dst, dst_tok, src_tok):
        for half in range(2):
            bank = PB[1]
            for q in range(4):
                kc = half * 4 + q
                dg = diag[kc % 2]
                P.op("dve", lambda e, dg=dg, kc=kc: e.tensor_scalar(out=dg[:], in0=identf[:], scalar1=col_fn(kc),
                                                                     scalar2=None, op0=ALU.mult),
                     reads=["identf", src_tok], writes=[("diag", kc % 2)])
                P.op("pe", lambda e, dg=dg, q=q: e.matmul(bank[:, q * 128:(q + 1) * 128], lhsT=onesf[:], rhs=dg[:],
                                                           start=True, stop=True),
                     reads=["onesf", ("diag", kc % 2)], writes=[("ps", 1)])
            P.op("act", lambda e, half=half: e.activation(out=dst[:, half * 512:(half + 1) * 512], in_=bank[:, :], func=AF.Copy),
                 reads=[("ps", 1)], writes=[dst_tok])

    bcast_rows(lambda kc: fgT[:, kc:kc + 1], fg_bc, "fg_bc", "fgT")

    modps = PB[0]
    P.op("dve", lambda e: e.tensor_copy(out=cTb[:], in_=cT[:]), reads=["cT"], writes=["cTb"])
    def mod_dma(j):
        i = j % 8
        dma_in("stgx%d" % i, stgx[i][:, :, :], wada_v[:, :, j * 128:(j + 1) * 128], writes=[("stgx", i)])

    for j in range(8):
        mod_dma(j)
    for j in range(24):
        i = j % 8
        st = stgx[i]
        wab = xn[j % 2][:, :].rearrange("p (k n) -> p k n", n=128)
        P.op("dve", lambda e, st=st, wab=wab: e.tensor_copy(out=wab, in_=st[:, :, :]), reads=[("stgx", i)], writes=[("xn", j % 2)])

        def mm(e, wab=wab, j=j):
            r = None
            for kc in range(KC):
                r = e.matmul(modps[:, 2 * j:2 * j + 2], lhsT=wab[:, kc, :], rhs=cTb[:, kc, :],
                             start=(kc == 0), stop=(kc == KC - 1))
            return r
        P.op("pe", mm, reads=[("xn", j % 2), "cTb"], writes=[("ps", 0)])
        if j + 8 < 24:
            mod_dma(j + 8)
    modps_v = modps[:, 0:48].rearrange("p (j b) -> p j b", b=NSEQ)
    for b in range(NSEQ):
        P.op("dve", lambda e, b=b: e.tensor_tensor(out=modT[:, :, b], in0=modps_v[:, :, b], in1=badaT[:, :], op=ALU.add),
             reads=[("ps", 0), "badaT"], writes=["modT"])
        P.op("dve", lambda e, b=b: e.scalar_tensor_tensor(out=acoef[:, :, b], in0=modT[:, 8:16, b], scalar=1.0,
                                                           in1=ngT[:, :], op0=ALU.add, op1=ALU.mult),
             reads=["modT", "ngT"], writes=["acoef"])

    def w5_pieces(cbs):
        pcs = []
        for cb in cbs:
            pcs.append((wosb_v[:, :, cb * 128:(cb + 1) * 128], 4, 128, wosb[:, :, cb * 128:(cb + 1) * 128], [("wosb", cb)]))
            pcs.append((wofox_v[:, :, cb * 128:(cb + 1) * 128], 4, 128, wofox[:, :, cb * 128:(cb + 1) * 128], [("wofox", cb)]))
            pcs.append((wout_v[:, :, cb * 128:(cb + 1) * 128], 8, 128, wout[:, :, cb * 128:(cb + 1) * 128], [("wout", cb)]))
        return pcs


    out_i = [0]

    for sq in range(NSEQ):
        row0 = sq * S
        P.op("pool", lambda e: e.memset(VV[:, :, 0, 64:128], 1.0), writes=["VVones"])
        P.op("pool", lambda e: e.memset(VV[:, :, 1, 0:64], 1.0), writes=["VVones"])
        P.op("pool", lambda e: e.memset(Pbuf[0][:, 1025:1026], 1.0), writes=["Pone"])
        P.op("pool", lambda e: e.memset(Pbuf[1][:, 1025:1026], 1.0), writes=["Pone"])
        P.op("pool", lambda e: e.memset(Pbuf[2][:, 1025:1026], 1.0), writes=["Pone"])
        i = stg_i[0] % NSTG
        stg_i[0] += 1
        dma_in("stg%d" % i, stg[i][:, :, 0:8], win_v[:, :, OFF_F:OFF_F + 8], writes=[("stg", i)])
        P.op("pool", lambda e, i=i: e.tensor_copy(out=wfw[:], in_=stg[i][:, :, 0:8]), reads=[("stg", i)], writes=["wfw"])

        def hT_toks(tg):
            r = []
            for tb in range(4 * tg, 4 * tg + 4):
                r += [("hT", tb)]
            return r

        def x_dma(tb):
            xi = tb % 4
            dma_in("xt%d" % xi, xt4[xi][:], x_d[row0 + tb * 128: row0 + (tb + 1) * 128, :], writes=[("xt", xi)])

        def norm0(tb, sq=sq):
            par = tb % 2
            xi = tb % 4
            xb = xt4[xi]
            xnb = xn[par]
            s0 = 3 * par
            xextra = YG_TOKS if xi >= 2 else []
            P.op("act", lambda e: e.activation(out=gbuf[0][:], in_=xb[:], func=AF.Square, accum_out=stat[:, s0:s0 + 1]),
                 reads=[("xt", xi)] + xextra, writes=[("g", 0), ("ss", par)])
            P.op("act", lambda e: e.activation(out=stat[:, s0 + 1:s0 + 2], in_=stat[:, s0:s0 + 1], func=AF.Ln, scale=1.0 / D, bias=eps_t[:, 0:1]),
                 reads=[("ss", par), "eps"], writes=[("sd", par)])
            P.op("act", lambda e: e.activation(out=stat[:, s0 + 2:s0 + 3], in_=stat[:, s0 + 1:s0 + 2], func=AF.Exp, scale=-0.5),
                 reads=[("sd", par)], writes=[("rstd", par)])
            P.op("dve", lambda e: e.tensor_scalar(out=xnb[:], in0=xb[:], scalar1=stat[:, s0 + 2:s0 + 3], scalar2=None, op0=ALU.mult),
                 reads=[("xt", xi), ("rstd", par)] + xextra, writes=[("xn", par)])
            if tb + 4 < NB:
                x_dma(tb + 4)

        def norm1(tb, sq=sq):
            par = tb % 2
            xnb = xn[par]
            tv = TB[par][:, :].rearrange("p (k t) -> p k t", t=128)

            def tr(e):
                r = None
                for kc in range(KC):
                    r = e.transpose(out=tv[:, kc, :], in_=xnb[:, kc * 128:(kc + 1) * 128], identity=ident[:])
                return r
            P.op("pe", tr, reads=[("xn", par), "ident"], writes=[("tb", par)])
            if False:
                def ev(e):
                    r = None
                    for kc in range(KC):
                        r = e.activation(out=hT[:, kc, tb * 128:(tb + 1) * 128], in_=tv[:, kc, :], func=AF.Identity,
                                         scale=acoef[:, kc, sq:sq + 1], bias=modT[:, kc, sq:sq + 1])
                    return r
                P.op("act", ev, reads=[("tb", par), "acoef", "modT"], writes=[("hT", tb)])
            else:
                def ev(e):
                    r = None
                    for kc in range(KC):
                        r = e.tensor_scalar(out=hT[:, kc, tb * 128:(tb + 1) * 128], in0=tv[:, kc, :],
                                            scalar1=acoef[:, kc, sq:sq + 1], scalar2=modT[:, kc, sq:sq + 1],
                                            op0=ALU.mult, op1=ALU.add)
                    return r
                P.op("dve", ev, reads=[("tb", par), "acoef", "modT"], writes=[("hT", tb)])

        def f_phase(tg):
            cols = slice(tg * 512, (tg + 1) * 512)
            fbank = PB[1]

            def fmm(e, cols=cols):
                r = None
                for kc in range(KC):
                    r = e.matmul(fbank[0:8, :], lhsT=wfw[:, kc, :], rhs=hT[:, kc, cols], start=(kc == 0), stop=(kc == KC - 1))
                return r
            P.op("pe", fmm, reads=["wfw"] + hT_toks(tg), writes=[("ps", 1)])
            P.op("act", lambda e: e.activation(out=ftmp[0][:], in_=fbank[0:8, :], func=AF.Exp, bias=negb[:, 0:1], scale=-1.0),
                 reads=[("ps", 1), "negb"] + FT, writes=["f0"])
            P.op("act", lambda e: e.activation(out=ftmp[0][:], in_=ftmp[0][:], func=AF.Ln, bias=ones_c[0:8, 0:1], scale=1.0),
                 reads=["f0", "ones_c"] + FT, writes=["f0"])
            Fc = ftmp[1 + tg % 2]
            Fp = ftmp[1 + (tg + 1) % 2]
            init = 0.0 if tg == 0 else Fp[:, 511:512]
            P.op("dve", lambda e, Fc=Fc, init=init: e.tensor_tensor_scan(out=Fc[:], data0=ftmp[0][:], data1=ftmp[0][:], initial=init,
                                                                          op0=ALU.add, op1=ALU.max),
                 reads=["f0", ("F", (tg + 1) % 2)] + FT, writes=[("F", tg % 2)])
            P.op("dve", lambda e, Fc=Fc, cols=cols: e.tensor_scalar(out=fsp[0:8, cols], in0=Fc[:], scalar1=-8.0, scalar2=None, op0=ALU.mult),
                 reads=[("F", tg % 2)], writes=[("fsp", tg)])
            P.op("dve", lambda e, Fc=Fc, cols=cols: e.scalar_tensor_tensor(out=ftmp[3][:], in0=Fc[:], scalar=-8.0, in1=fsp[0:8, cols],
                                                                            op0=ALU.mult, op1=ALU.subtract),
                 reads=[("F", tg % 2), ("fsp", tg)] + FT, writes=["r1"])
            P.op("dve", lambda e: e.tensor_copy(out=fmid[:], in_=ftmp[3][:]), reads=["r1"] + FT, writes=["fmid"])
            P.op("dve", lambda e, cols=cols: e.tensor_copy(out=fsp[32:40, cols], in_=fmid[:]), reads=["fmid"], writes=[("fsp1", tg)])
            P.op("dve", lambda e, cols=cols: e.tensor_tensor(out=fsp[64:72, cols], in0=ftmp[3][:], in1=fmid[:], op=ALU.subtract),
                 reads=["r1", "fmid"], writes=[("fsp2", tg)])
        fsp_toks = []
        for tg in range(NTG):
            fsp_toks += [("fsp", tg), ("fsp1", tg), ("fsp2", tg)]

        def load_chunk_weights(offs):
            for nm, (dst, off) in offs.items():
                load_piece(win_v[:, :, off:off + 128], KC, 128, dst[:, :, :], [nm])

        def proj_fm(wt, wtok, tg, bank_i):
            bank = PB[bank_i] if bank_i < 10 else TBF[bank_i - 10]
            cols = slice(tg * 512, (tg + 1) * 512)

            def mm(e):
                r = None
                for kc in range(KC):
                    r = e.matmul(bank[:, :], lhsT=wt[:, kc, :], rhs=hT[:, kc, cols], start=(kc == 0), stop=(kc == KC - 1))
                return r
            P.op("pe", mm, reads=[wtok] + hT_toks(tg), writes=[("ps", bank_i) if bank_i < 10 else ("tb", bank_i - 10)])
            return bank

        def proj_v(c):
            for g4 in range(NB // 4):
                bank = TBF[g4 % 2]

                def mm(e, g4=g4, bank=bank):
                    r = None
                    for q in range(4):
                        sbk = g4 * 4 + q
                        for kc in range(KC):
                            r = e.matmul(bank[:, q * 128:(q + 1) * 128], lhsT=hT[:, kc, sbk * 128:(sbk + 1) * 128], rhs=wv[:, kc, :],
                                         start=(kc == 0), stop=(kc == KC - 1))
                    return r
                P.op("pe", mm, reads=["wv"] + hT_toks(g4), writes=[("tb", g4 % 2)])
                bv = bank[:, :].rearrange("p (q n) -> p q n", n=128)

                def ev(e, g4=g4, bv=bv):
                    e.activation(out=VV[:, g4 * 4:(g4 + 1) * 4, 0, 0:64], in_=bv[:, :, 0:64], func=AF.Copy)
                    return e.activation(out=VV[:, g4 * 4:(g4 + 1) * 4, 1, 64:128], in_=bv[:, :, 64:128], func=AF.Copy)
                P.op("act", ev, reads=[("tb", g4 % 2), "VVones"], writes=[("VV", g4)])

        VV_toks = [("VV", g) for g in range(NB // 4)]

        rowc = [0]

        deferred = []

        def flush_deferred():
            while deferred:
                deferred.pop(0)()

        def a_items(c):
            par = c % 2
            qTp = QB[:, par, :]
            kTp = KB[:, par, :]
            zTp = zTa[par]
            Vp = VVa[par]
            items = staged_items(chunk_pieces(OFF_QA + c * 128, OFF_KA + c * 128, OFF_VA + c * 128, OFF_ZA + c * 128), "act" if c == 0 else "dve")

            def mk(wt, wtok, tg, dst, tokname, func):
                def it():
                    cols = slice(tg * 512, (tg + 1) * 512)
                    bk = proj_fm(wt, wtok, tg, 5)

                    def ev():
                        if func is None:
                            zs = zsig[tg % 2]
                            P.op("act", lambda e: e.activation(out=zs[:], in_=bk[:, :], func=AF.Sigmoid),
                                 reads=[("ps", 5)], writes=[("zsig", 0)])
                            P.op("dve", lambda e: e.tensor_tensor(out=dst[:, cols], in0=bk[:, :], in1=zs[:], op=ALU.mult),
                                 reads=[("ps", 5), ("zsig", 0)], writes=[(tokname, par, tg)])
                        else:
                            P.op("act", lambda e: e.activation(out=dst[:, cols], in_=bk[:, :], func=func),
                                 reads=[("ps", 5)], writes=[(tokname, par, tg)])
                    deferred.append(ev)
                return it
            for tg in range(NTG):
                items.append(mk(wq, "wq", tg, qTp, "qa", AF.Copy))
                items.append(mk(wk, "wk", tg, kTp, "ka", AF.Copy))
                items.append(mk(wz, "wz", tg, zTp, "za", None))

            def mkv(g4):
                def it():
                    bank = PB[5]

                    def mm(e):
                        r = None
                        for q in range(4):
                            sbk = g4 * 4 + q
                            for kc in range(KC):
                                r = e.matmul(bank[:, q * 128:(q + 1) * 128], lhsT=hT[:, kc, sbk * 128:(sbk + 1) * 128], rhs=wv[:, kc, :],
                                             start=(kc == 0), stop=(kc == KC - 1))
                        return r
                    P.op("pe", mm, reads=["wv"] + hT_toks(g4), writes=[("ps", 5)])
                    deferred.append(lambda: P.op("act", lambda e: e.activation(
                        out=Vp[:, g4 * 4:(g4 + 1) * 4, :, :], in_=bank[:, :].rearrange("p (q h n) -> p q h n", h=2, n=64), func=AF.Copy),
                        reads=[("ps", 5)], writes=[("va", par, g4), ("xt", par)]))
                return it
            for g4 in range(NB // 4):
                items.append(mkv(g4))
            return items

        items0 = a_items(0)
        for tb in range(4):
            x_dma(tb)
        items0[0]()
        for step in range(NB + 1):
            if step < NB:
                norm0(step)
            if step >= 1:
                norm1(step - 1)
            if 1 <= step <= 4:
                items0[step]()
            if step >= 1 and step % 4 == 0:
                tg = step // 4 - 1
                f_phase(tg)
                for it in items0[5 + 3 * tg: 7 + 3 * tg]:
                    it()
                    flush_deferred()
        for tg in range(NTG):
            items0[7 + 3 * tg]()
            flush_deferred()
        for it in items0[17:]:
            it()
            flush_deferred()
        bcast_rows(lambda kc, sq=sq: modT[:, 16 + kc, sq:sq + 1], gate_bc, "gate_bc", "modT")
        def b0_items():
            items = []

            def mkz(tg):
                def it():
                    cols = slice(tg * 512, (tg + 1) * 512)
                    bk = proj_fm(wz, "wz", tg, 5)
                    zs = zsig[tg % 2]

                    def ev():
                        P.op("act", lambda e: e.activation(out=zs[:], in_=bk[:, :], func=AF.Sigmoid),
                             reads=[("ps", 5)], writes=[("zsig", 0)])
                        P.op("dve", lambda e: e.tensor_tensor(out=zT[:, cols], in0=bk[:, :], in1=zs[:], op=ALU.mult),
                             reads=[("ps", 5), ("zsig", 0)], writes=[("z", tg), ("za", 0, tg)])
                    deferred.append(ev)
                return it

            def mkv(g4):
                def it():
                    bank = PB[5]

                    def mm(e):
                        r = None
                        for q in range(4):
                            sbk = g4 * 4 + q
                            for kc in range(KC):
                                r = e.matmul(bank[:, q * 128:(q + 1) * 128], lhsT=hT[:, kc, sbk * 128:(sbk + 1) * 128], rhs=wv[:, kc, :],
                                             start=(kc == 0), stop=(kc == KC - 1))
                        return r
                    P.op("pe", mm, reads=["wv"] + hT_toks(g4), writes=[("ps", 5)])
                    bv = bank[:, :].rearrange("p (q n) -> p q n", n=128)

                    def ev(e):
                        e.activation(out=VV[:, g4 * 4:(g4 + 1) * 4, 0, 0:64], in_=bv[:, :, 0:64], func=AF.Copy)
                        return e.activation(out=VV[:, g4 * 4:(g4 + 1) * 4, 1, 64:128], in_=bv[:, :, 64:128], func=AF.Copy)
                    deferred.append(lambda: P.op("act", ev, reads=[("ps", 5), "VVones"], writes=[("VV", g4)]))
                return it
            for tg in range(NTG):
                items.append(mkz(tg))
            for g4 in range(NB // 4):
                items.append(mkv(g4))
            return items

        def run_chunk_A(c):
            par = c % 2
            qT = QB[:, par, :]
            kT = KB[:, par, :]
            zTc = zTa[par]
            Vc = VVa[par]
            bg = a_items(c + 1) if c < 3 else (staged_items(chunk_pieces(OFF_QB, OFF_KB, OFF_VB, OFF_ZB), "dve") + b0_items())
            if sq == 0:
                bg = bg + staged_items(w5_pieces([2 * c, 2 * c + 1]), "dve")
            k_toks = [("ka", par, tg) for tg in range(NTG)]
            Va_toks = [("va", par, g) for g in range(NB // 4)]
            rows = []
            for tg in range(NTG):
                for hh in range(2):
                    for qb in range(4 * tg, 4 * tg + 4):
                        ncols = (qb + 1) * 128
                        segs = []
                        hi = ncols
                        while hi > 0:
                            lo = max(0, hi - 1024)
                            segs.append((lo, hi))
                            hi = lo
                        pv_i = [0]
                        for si, (lo, hi) in enumerate(segs):
                            rows.append(dict(tg=tg, hh=hh, qb=qb, si=si, lo=lo, hi=hi, segs=segs, pv_i=pv_i,
                                             last=(hh == 1 and qb == 4 * tg + 3 and si == len(segs) - 1)))
            ybank = PB[4]
            p0rot = [0]
            p0of = {}

            def stA0(rw):
                r = rowc[0]
                rowc[0] += 1
                rw["r"] = r
                tg, hh, qb, si, lo, hi = rw["tg"], rw["hh"], rw["qb"], rw["si"], rw["lo"], rw["hi"]
                hb = 64 * hh
                n = hi - lo
                zb = r % 2
                zbanks = (PB[2 * zb], PB[2 * zb + 1])
                gb = gbuf[r % 2]
                ztoks = [("ps", 2 * zb), ("ps", 2 * zb + 1)]

                def qk(e):
                    rr = None
                    for p0 in range(0, n, 512):
                        w_ = min(512, n - p0)
                        bank = zbanks[p0 // 512]
                        isdiag = (si == 0 and p0 + w_ == n)
                        rr = e.matmul(bank[:, 0:w_], lhsT=qT[hb:hb + 64, qb * 128:(qb + 1) * 128],
                                      rhs=kT[hb:hb + 64, lo + p0: lo + p0 + w_], start=True, stop=not isdiag)
                        if isdiag:
                            rr = e.matmul(bank[:, w_ - 128:w_], lhsT=ident[:], rhs=maskA[:], start=False, stop=True)
                    return rr
                P.op("pe", qk, reads=[("qa", par, tg)] + k_toks + ["ident", "maskA"], writes=ztoks)

                def sg(e):
                    return e.activation(out=gb[:, 0:n], in_=ZP[zb][:, 0:n], func=AF.Sigmoid, scale=-0.125)
                P.op("act", sg, reads=ztoks, writes=[("g", r % 2)])
                bb = bbuf[r % 3]

                def sb_(e):
                    return e.activation(out=bb[:, 0:n], in_=ZP[zb][:, 0:n], func=AF.Sigmoid, scale=0.125)
                P.op("act", sb_, reads=ztoks, writes=[("b", r % 3)])

            def stA1a(rw):
                r, si, lo, hi, segs, qb = rw["r"], rw["si"], rw["lo"], rw["hi"], rw["segs"], rw["qb"]
                n = hi - lo
                gb = gbuf[r % 2]
                if si == 0:
                    pidx = p0rot[0] % 3
                    p0rot[0] += 1
                    p0of[qb] = pidx
                    init = 1.0
                    sc_reads = [("g", r % 2), "Pone"]
                else:
                    pprev = p0of[qb]
                    n0 = segs[0][1] - segs[0][0]
                    pidx = 3
                    P.op("dve", lambda e: e.tensor_copy(out=carry[:, 0:1], in_=Pbuf[pprev][:, 1025 - n0:1026 - n0]),
                         reads=[("P", pprev)], writes=[("Pc", 2)])
                    P.op("dve", lambda e: e.tensor_copy(out=Pbuf[3][:, 1025:1026], in_=carry[:, 0:1]),
                         reads=[("Pc", 2)], writes=[("Pc2", 2)])
                    init = carry[:, 0:1]
                    sc_reads = [("g", r % 2), ("Pc", 2)]
                rw["pidx"] = pidx
                Pb_ = Pbuf[pidx]
                P.op("dve", lambda e: e.tensor_tensor_scan(
                    out=Pb_[:, 1025 - n:1025][:, ::-1], data0=gb[:, 0:n][:, ::-1], data1=gb[:, 0:n][:, ::-1],
                    initial=init, op0=ALU.mult, op1=ALU.min),
                    reads=sc_reads, writes=[("P", pidx)])

            def stA1b(rw):
                r, lo, hi, pidx = rw["r"], rw["lo"], rw["hi"], rw["pidx"]
                n = hi - lo
                bb = bbuf[r % 3]
                wb = wbuf[r % 3]
                Pb_ = Pbuf[pidx]
                P.op("dve", lambda e: e.tensor_tensor(
                    out=wb[:, 0:n], in0=Pb_[:, 1026 - n:1026], in1=bb[:, 0:n], op=ALU.mult),
                    reads=[("P", pidx), ("Pc2", 2), "Pone", ("b", r % 3)], writes=[("w", r % 3)])

            def stA2(rw):
                r, lo, hi = rw["r"], rw["lo"], rw["hi"]
                n = hi - lo
                wb = wbuf[r % 3]
                tbk = TB[r % 2]
                wtb = wTb[r % 3]

                def trw(e):
                    rr = None
                    for j in range(n // 128):
                        rr = e.transpose(out=tbk[:, j * 128:(j + 1) * 128], in_=wb[:, j * 128:(j + 1) * 128], identity=ident[:])
                    return rr
                P.op("pe", trw, reads=[("w", r % 3), "ident"], writes=[("tb", r % 2)])
                P.op("act", lambda e: e.activation(out=wtb[:, 0:n], in_=tbk[:, 0:n], func=AF.Copy),
                     reads=[("tb", r % 2)], writes=[("wT", r % 3)])

            def stA3(rw, c=c):
                r, tg, hh, qb, lo, hi, pv_i = rw["r"], rw["tg"], rw["hh"], rw["qb"], rw["lo"], rw["hi"], rw["pv_i"]
                hb = 64 * hh
                n = hi - lo
                nkb = qb + 1
                wtb = wTb[r % 3]

                def pv(e):
                    rr = None
                    for j in range(n // 128):
                        kbk = lo // 128 + j
                        first = (pv_i[0] == 0)
                        pv_i[0] += 1
                        last = (pv_i[0] == nkb)
                        rr = e.matmul(ybank[hb:hb + 64, (qb % 4) * 128:(qb % 4 + 1) * 128],
                                      lhsT=Vc[:, kbk, hh, :], rhs=wtb[:, j * 128:(j + 1) * 128],
                                      start=first, stop=last)
                    return rr
                P.op("pe", pv, reads=[("wT", r % 3)] + Va_toks, writes=[("ps", 4)])
                if rw["last"]:
                    P.op("dve", lambda e: e.tensor_tensor(
                        out=ygA[:, c, tg * 512:(tg + 1) * 512], in0=ybank[:, :], in1=zTc[:, tg * 512:(tg + 1) * 512], op=ALU.mult),
                        reads=[("ps", 4), ("za", par, tg)], writes=[("ygA", tg)])

            return rows, (stA0, stA1a, stA1b, stA2, stA3), bg

        stream = []
        chunk_start = {}
        bgs = {}
        nrows = {}
        for c in range(4):
            rows_c, fns_c, bg_c = run_chunk_A(c)
            chunk_start[len(stream)] = c
            bgs[c] = bg_c
            nrows[c] = len(rows_c)
            stream += [(rw, fns_c) for rw in rows_c]
        nst = len(stream)
        cur_bg = []
        local0 = 0
        nr_c = 1
        for step in range(nst + 6):
            flush_deferred()
            if step < nst:
                if step in chunk_start:
                    while cur_bg:
                        cur_bg.pop(0)()
                        flush_deferred()
                    cur_bg = bgs[chunk_start[step]]
                    nr_c = nrows[chunk_start[step]]
                    local0 = step
                rw, fns = stream[step]
                fns[0](rw)
            for k, d in ((1, 1), (2, 2), (3, 4), (4, 6)):
                if 0 <= step - d < nst:
                    rw, fns = stream[step - d]
                    fns[k](rw)
            ls = step - local0
            if cur_bg and (ls == 0 or (ls >= 3 and (ls % 2 == 1 or 2 * len(cur_bg) > (nr_c + 3 - ls)))):
                cur_bg.pop(0)()
        flush_deferred()
        while cur_bg:
            cur_bg.pop(0)()
            flush_deferred()

        zrot = [0]
        yrot = [0]
        P.op("pool", lambda e: e.memset(QB[64:70, :, :], -1.0),
             writes=["QBaug"] + [("q", tg) for tg in range(NTG)] + [("qa", p_, tg) for p_ in range(2) for tg in range(NTG)])
        P.op("pool", lambda e: e.memset(KB[64:70, :, :], 1.0),
             writes=["KBaug"] + [("k", tg) for tg in range(NTG)] + [("ka", p_, tg) for p_ in range(2) for tg in range(NTG)])
        for c in range(4):
            for hh in range(2):
                for j in range(3):
                    hd = 2 * c + hh
                    dma_in("augq%d%d" % (hh, j), QB[64 + j:65 + j, hh, :], fsp[hd + 32 * j:hd + 32 * j + 1, :], reads=fsp_toks + ["QBaug"],
                           writes=[("qaug", hh, j)], eng="sp")
                    dma_in("augk%d%d" % (hh, j), KB[67 + j:68 + j, hh, :], fsp[hd + 32 * j:hd + 32 * j + 1, :], reads=fsp_toks + ["KBaug"],
                           writes=[("kaug", hh, j)], eng="sp")
            for tg in range(NTG):
                cols = slice(tg * 512, (tg + 1) * 512)
                bk = proj_fm(wq, "wq", tg, 10)

                def evq(e, bk=bk, cols=cols):
                    e.activation(out=QB[0:64, 0, cols], in_=bk[0:64, :], func=AF.Copy)
                    return e.activation(out=QB[0:64, 1, cols], in_=bk[64:128, :], func=AF.Copy)
                P.op("act", evq, reads=[("tb", 0)], writes=[("q", tg)])
                bk = proj_fm(wk, "wk", tg, 11)

                def evk(e, bk=bk, cols=cols):
                    e.activation(out=KB[0:64, 0, cols], in_=bk[0:64, :], func=AF.Copy)
                    return e.activation(out=KB[0:64, 1, cols], in_=bk[64:128, :], func=AF.Copy)
                P.op("act", evk, reads=[("tb", 1)], writes=[("k", tg)])
            for tg in range(NTG if c > 0 else 0):
                cols = slice(tg * 512, (tg + 1) * 512)
                bk = proj_fm(wz, "wz", tg, 10 + tg % 2)
                P.op("act", lambda e, bk=bk, cols=cols: e.activation(out=zT[:, cols], in_=bk[:, :], func=AF.Silu),
                     reads=[("tb", tg % 2)], writes=[("z", tg), ("za", 0, tg)])
            if c > 0:
                proj_v(c)
            pcs = []
            if c < 3:
                pcs += chunk_pieces(OFF_QB + (c + 1) * 128, OFF_KB + (c + 1) * 128, OFF_VB + (c + 1) * 128, OFF_ZB + (c + 1) * 128)
            for cb in range(4 * c, 4 * c + 4):
                pcs.append((win_v[:, :, OFF_G + cb * 128: OFF_G + (cb + 1) * 128], KC, 128, wgp[cb][:, :, :],
                            [("wg", cb)] + PARTB_TOKS))
            bgB = staged_items(pcs, "dve")
            aug_toks = [("qaug", hh, j) for hh in range(2) for j in range(3)] + [("kaug", hh, j) for hh in range(2) for j in range(3)]
            k_toks = [("k", tg) for tg in range(NTG)]
            rows = []
            for tg in range(NTG):
                for hh in range(2):
                    ybi = yrot[0] % 3
                    yrot[0] += 1
                    nkb = 4 * tg + 4
                    for kbk in range(nkb):
                        rows.append(dict(tg=tg, hh=hh, kbk=kbk, nkb=nkb, ybi=ybi))

            def stB0(rw):
                tg, hh, kbk = rw["tg"], rw["hh"], rw["kbk"]
                if kbk < 4 * tg:
                    c0 = 0
                    isdiag = False
                else:
                    c0 = (kbk - 4 * tg) * 128
                    isdiag = True
                rw["c0"] = c0
                zi = zrot[0] % 3
                pi = zrot[0] % 4
                zrot[0] += 1
                rw["pi"] = pi
                zbank = PB[zi]
                pb_ = pbuf[pi]

                def qk(e):
                    rr = e.matmul(zbank[:, c0:512], lhsT=KB[0:70, hh, kbk * 128:(kbk + 1) * 128],
                                  rhs=QB[0:70, hh, tg * 512 + c0:(tg + 1) * 512], start=True, stop=not isdiag)
                    if isdiag:
                        rr = e.matmul(zbank[:, c0:c0 + 128], lhsT=ident[:], rhs=maskB[:], start=False, stop=True)
                    return rr
                P.op("pe", qk, reads=[("q", tg)] + k_toks + aug_toks + ["ident", "maskB", "QBaug", "KBaug"], writes=[("ps", zi)])
                rw["zi"] = zi

            def stB0b(rw):
                c0, zi, pi = rw["c0"], rw["zi"], rw["pi"]
                zbank = PB[zi]
                pb_ = pbuf[pi]
                P.op("act", lambda e: e.activation(out=pb_[:, c0:512], in_=zbank[:, c0:512], func=AF.Exp, scale=0.125),
                     reads=[("ps", zi)], writes=[("p", pi)])

            def stB1(rw, c=c):
                tg, hh, kbk, nkb, ybi, c0, pi = rw["tg"], rw["hh"], rw["kbk"], rw["nkb"], rw["ybi"], rw["c0"], rw["pi"]
                ybank = PB[3 + ybi]
                pb_ = pbuf[pi]
                P.op("pe", lambda e: e.matmul(
                    ybank[:, c0:512], lhsT=VV[:, kbk, hh, :], rhs=pb_[:, c0:512], start=(kbk == 0), stop=(kbk == nkb - 1)),
                    reads=[("p", pi)] + VV_toks + ["VVones"], writes=[("ps", 3 + ybi)])
                if kbk == nkb - 1:
                    ys = slice(0, 64) if hh == 0 else slice(64, 128)
                    ds_ = slice(64, 128) if hh == 0 else slice(0, 64)
                    P.op("dve", lambda e: e.reciprocal(out=recb[ds_, :], in_=ybank[ds_, :]),
                         reads=[("ps", 3 + ybi)], writes=["rec"])
                    P.op("dve", lambda e: e.tensor_tensor(out=tmpb[ys, :], in0=ybank[ys, :], in1=recb[ds_, :], op=ALU.mult),
                         reads=[("ps", 3 + ybi), "rec"], writes=["tmpb"])
                    P.op("dve", lambda e: e.tensor_tensor(out=ygB[ys, c, tg * 512:(tg + 1) * 512], in0=tmpb[ys, :],
                                                          in1=zT[ys, tg * 512:(tg + 1) * 512], op=ALU.mult),
                         reads=["tmpb", ("z", tg)], writes=[("ygB", tg, hh)])

            nr = len(rows)
            for step in range(nr + 3):
                if step < nr:
                    stB0(rows[step])
                if 0 <= step - 1 < nr:
                    stB0b(rows[step - 1])
                if 0 <= step - 3 < nr:
                    stB1(rows[step - 3])
                if bgB and step % 4 == 1:
                    bgB.pop(0)()
            while bgB:
                bgB.pop(0)()

        P.barrier()
        def o_phase(tg, mt):
            mT_toks = [("mT", tg % 2, j) for j in range(8)]
            pending_tail = []
            for qq in range(4):
                tb = tg * 4 + qq
                oi = out_i[0] % 2
                out_i[0] += 1
                xb = xt[oi]
                xrb = xr[oi]
                ob = obuf[oi]
                dma_in("xt%d" % oi, xb[:], x_d[row0 + tb * 128: row0 + (tb + 1) * 128, :], writes=[("xt", oi)])
                for nh in range(2):
                    obank = PB[4 + nh]
                    ncol = slice(nh * 512, (nh + 1) * 512)

                    def mm_o(e, mt=mt, qq=qq, ncol=ncol, obank=obank):
                        rr = None
                        for kc in range(KC):
                            rr = e.matmul(obank[:, :], lhsT=mt[:, kc, qq * 128:(qq + 1) * 128], rhs=wout[:, kc, ncol], start=(kc == 0), stop=(kc == KC - 1))
                        return rr
                    P.op("pe", mm_o, reads=mT_toks + [("wout", cb_) for cb_ in range(8)], writes=[("ps", 4 + nh)])
                    P.op("dve", lambda e, obank=obank, ncol=ncol, xrb=xrb: e.tensor_tensor(out=xrb[:, ncol], in0=obank[:, :], in1=gate_bc[:, ncol], op=ALU.mult),
                         reads=[("ps", 4 + nh), "gate_bc"], writes=[("xr", oi, nh), ("xr2", oi, nh)])
                s5 = 3 * oi
                for nh in range(2):
                    ncol = slice(nh * 512, (nh + 1) * 512)
                    P.op("dve", lambda e, ncol=ncol, xrb=xrb, xb=xb: e.tensor_tensor(out=xrb[:, ncol], in0=xrb[:, ncol], in1=xb[:, ncol], op=ALU.add),
                         reads=[("xr", oi, nh), ("xt", oi)], writes=[("xr2", oi, nh)])
                P.op("act", lambda e, xrb=xrb, ob=ob, s5=s5: e.activation(out=ob[:], in_=xrb[:], func=AF.Square, accum_out=stat[:, s5:s5 + 1]),
                     reads=[("xr2", oi, 0), ("xr2", oi, 1)], writes=[("ob", oi), ("ss5", oi)])
                P.op("act", lambda e, s5=s5: e.activation(out=stat[:, s5 + 1:s5 + 2], in_=stat[:, s5:s5 + 1], func=AF.Ln, scale=1.0 / D, bias=eps_t[:, 0:1]),
                     reads=[("ss5", oi), "eps"], writes=[("sd5", oi)])
                P.op("act", lambda e, s5=s5: e.activation(out=stat[:, s5 + 2:s5 + 3], in_=stat[:, s5 + 1:s5 + 2], func=AF.Exp, scale=-0.5),
                     reads=[("sd5", oi)], writes=[("rstd5", oi)])

                while pending_tail:
                    pending_tail.pop(0)()

                def tail(oi=oi, xrb=xrb, ob=ob, tb=tb, s5=s5):
                    P.op("dve", lambda e: e.scalar_tensor_tensor(out=ob[:], in0=xrb[:], scalar=stat[:, s5 + 2:s5 + 3], in1=fg_bc[:],
                                                                 op0=ALU.mult, op1=ALU.mult),
                         reads=[("xr2", oi, 0), ("xr2", oi, 1), ("rstd5", oi), "fg_bc"], writes=[("ob", oi)])
                    dma_in("out%d" % oi, out_d[row0 + tb * 128: row0 + (tb + 1) * 128, :], ob[:], reads=[("ob", oi)], writes=[("outd", tb, sq)], eng="pool")
                pending_tail.append(tail)
            while pending_tail:
                pending_tail.pop(0)()
        for tg in range(NTG):
            cols = slice(tg * 512, (tg + 1) * 512)
            mt = mT[tg % 2]
            for j in range(8):
                jb = (tg * 8 + j) % 2
                fcols = slice(j * 128, (j + 1) * 128)

                def mm_pa(e, fcols=fcols, cols=cols):
                    rr = None
                    for kc in range(4):
                        rr = e.matmul(PB[0][:, :], lhsT=wosb[:, kc, fcols], rhs=ygA[:, kc, cols], start=(kc == 0), stop=(kc == 3))
                    return rr

                def mm_pb(e, fcols=fcols, cols=cols):
                    rr = None
                    for kc in range(4):
                        rr = e.matmul(PB[1][:, :], lhsT=wofox[:, kc, fcols], rhs=ygB[:, kc, cols], start=(kc == 0), stop=(kc == 3))
                    return rr

                def mm_g(e, j=j, cols=cols, which=0):
                    rr = None
                    for kc in range(KC):
                        rr = e.matmul(PB[2 + which][:, :], lhsT=wgp[which * 8 + j][:, kc, :], rhs=hT[:, kc, cols],
                                      start=(kc == 0), stop=(kc == KC - 1))
                    return rr
                P.op("pe", lambda e, f=mm_g: f(e, which=0), reads=[("wg", j)] + hT_toks(tg), writes=[("ps", 2)])
                P.op("act", lambda e, j=j, jb=jb: e.activation(out=G0s[jb][:], in_=PB[2][:, :], func=AF.Sigmoid, bias=bgT[:, j:j + 1], scale=1.0),
                     reads=[("ps", 2), "bgT"], writes=[("G0", jb)])
                P.op("pe", lambda e, f=mm_g: f(e, which=1), reads=[("wg", 8 + j)] + hT_toks(tg), writes=[("ps", 3)])
                P.op("act", lambda e, j=j, jb=jb: e.activation(out=G1s[jb][:], in_=PB[3][:, :], func=AF.Sigmoid, bias=bgT[:, 8 + j:9 + j], scale=1.0),
                     reads=[("ps", 3), "bgT"], writes=[("G1", jb)])
                P.op("pe", mm_pa, reads=[("wosb", j), ("ygA", tg)], writes=[("ps", 0)])
                P.op("pe", mm_pb, reads=[("wofox", j), ("ygB", tg, 0), ("ygB", tg, 1)], writes=[("ps", 1)])
                P.op("dve", lambda e, jb=jb: e.tensor_tensor(out=t0b[jb][:], in0=PB[0][:, :], in1=G0s[jb][:], op=ALU.mult),
                     reads=[("ps", 0), ("G0", jb)], writes=[("t0", jb)])
                P.op("dve", lambda e, jb=jb: e.tensor_tensor(out=t1b[jb][:], in0=PB[1][:, :], in1=G1s[jb][:], op=ALU.mult),
                     reads=[("ps", 1), ("G1", jb)], writes=[("t1", jb)])
                P.op("pool", lambda e, jb=jb, mt=mt, j=j: e.tensor_tensor(out=mt[:, j, :], in0=t0b[jb][:], in1=t1b[jb][:], op=ALU.add),
                     reads=[("t0", jb), ("t1", jb)], writes=[("mT", tg % 2, j)])
                if j == 1 and tg >= 1:
                    o_phase(tg - 1, mT[(tg - 1) % 2])
        o_phase(NTG - 1, mT[(NTG - 1) % 2])
        P.barrier()

    P.emit()
    return nc


def _build():
    return build_nc()


_NC_CACHE = {}


def _host_layout(inputs, core):
    b0 = core * NSEQ
    f32 = np.float32
    x = np.ascontiguousarray(inputs["x"][b0:b0 + NSEQ].reshape(NSEQ * S, D), dtype=f32)
    c = np.asarray(inputs["c"][b0:b0 + NSEQ], dtype=f32)
    cT = np.ascontiguousarray(c.reshape(NSEQ, KC, 128).transpose(2, 1, 0).reshape(128, KC * NSEQ))

    def fm(v, n):
        return np.ascontiguousarray(np.asarray(v, dtype=f32).reshape(n, 128).T)

    t = np.arange(128)
    maskA = np.where(t[None, :] >= t[:, None], MASKVAL, 0.0).astype(f32)
    maskB = np.where(t[:, None] > t[None, :], MASKVAL, 0.0).astype(f32)
    return {
        "x": x, "cT": cT,
        "w_ada": np.ascontiguousarray(inputs["w_ada"][0], dtype=f32),
        "badaT": fm(inputs["b_ada"][0], 24),
        "ngT": fm(inputs["norm_g"][0], KC),
        "w_in": np.ascontiguousarray(inputs["w_in"][0], dtype=f32),
        "bf": np.ascontiguousarray(np.asarray(inputs["b_forget"][0], dtype=f32).reshape(8, 1)),
        "w_o_sb": np.ascontiguousarray(inputs["w_o_sb"][0], dtype=f32),
        "w_o_fox": np.ascontiguousarray(inputs["w_o_fox"][0], dtype=f32),
        "bgT": fm(inputs["b_gate"][0], 16),
        "w_out": np.ascontiguousarray(inputs["w_out"][0], dtype=f32),
        "fgT": fm(inputs["final_g"], KC),
        "identf": np.eye(128, dtype=f32),
        "maskA": maskA, "maskB": maskB,
    }


def kernel(**inputs):
    inputs = {k: np.asarray(v) for k, v in inputs.items()}
    if "nc" not in _NC_CACHE:
        _NC_CACHE["nc"] = _build()
    nc = _NC_CACHE["nc"]
    in_maps = [_host_layout(inputs, i) for i in range(NCORES)]
    res = run_bass_kernel_spmd(nc, in_maps, core_ids=list(range(NCORES)))
    outs = [np.asarray(r["out"]).reshape(NSEQ, S, D) for r in res.results]
    return np.concatenate(outs, axis=0).astype(np.float32)
```

```python
import numpy as np
import concourse.bass as bass
import concourse.mybir as mybir
from concourse.bass_utils import run_bass_kernel_spmd

F32 = mybir.dt.float32
BF16 = mybir.dt.bfloat16
ALU = mybir.AluOpType
AF = mybir.ActivationFunctionType

NCORES = 8
S = 2048
D = 1024
NSEQ = 2
KC = 8
NB = S // 128
NTG = S // 512
EPS = 1e-6
OFF_QA, OFF_KA, OFF_VA, OFF_ZA = 0, 512, 1024, 1536
OFF_QB, OFF_KB, OFF_VB, OFF_ZB = 2048, 2560, 3072, 3584
OFF_F, OFF_G = 4096, 4104
D_IN = 6152
MASKVAL = -30000.0

ENGS = ["sp", "act", "dve", "pool", "pe"]


class Op:
    __slots__ = ("eng", "fn", "deps", "signal", "sigval", "dma_sem", "dma_val", "idx")

    def __init__(self, eng, fn):
        self.eng = eng
        self.fn = fn
        self.deps = []
        self.signal = False
        self.sigval = 0
        self.dma_sem = None
        self.dma_val = 0


class Prog:
    def __init__(self, nc):
        self.nc = nc
        self.ops = {e: [] for e in ENGS}
        self.last_writer = {}
        self.readers = {}
        self.dma_counts = {}
        self.all_dma_ops = []

    def op(self, eng, fn, reads=(), writes=(), dma_key=None):
        o = Op(eng, fn)
        deps = {}
        for t in reads:
            w = self.last_writer.get(t)
            if w is not None:
                deps[id(w)] = w
        for t in writes:
            w = self.last_writer.get(t)
            if w is not None:
                deps[id(w)] = w
            for r in self.readers.get(t, ()):
                deps[id(r)] = r
        for d in deps.values():
            if d is o:
                continue
            if d.dma_sem is None and d.eng == eng and eng in ("pe", "sp"):
                continue
            o.deps.append(d)
        for t in reads:
            self.readers.setdefault(t, []).append(o)
        for t in writes:
            self.last_writer[t] = o
            self.readers[t] = []
        if dma_key is not None:
            c = self.dma_counts.get(dma_key, 0) + 1
            self.dma_counts[dma_key] = c
            o.dma_sem = dma_key
            o.dma_val = 16 * c
            self.all_dma_ops.append(o)
        self.ops[eng].append(o)
        return o

    def barrier(self):
        lasts = []
        for e in ENGS:
            real = [o for o in self.ops[e] if o.fn is not None]
            if real:
                lasts.append(real[-1])
        dma_last = {}
        for o in self.all_dma_ops:
            dma_last[o.dma_sem] = o
        for e in ENGS:
            o = Op(e, None)
            for l in lasts:
                if l.eng != e or l.dma_sem is not None:
                    o.deps.append(l)
            for d in dma_last.values():
                o.deps.append(d)
            self.ops[e].append(o)

    def emit(self):
        nc = self.nc
        for e in ENGS:
            for o in self.ops[e]:
                for d in o.deps:
                    if d.dma_sem is None:
                        d.signal = True
        sems = {e: nc.alloc_semaphore("done_" + e) for e in ENGS}
        dsems = {k: nc.alloc_semaphore("dma_%s" % (str(k).replace(" ", "").replace("'", "").replace("(", "_").replace(")", "_").replace(",", "_")))
                 for k in self.dma_counts}
        for e in ENGS:
            c = 0
            for o in self.ops[e]:
                if o.signal:
                    c += 1
                    o.sigval = c
        prog = self

        def run(e, eng):
            waited = {}
            for o in prog.ops[e]:
                need = {}
                for d in o.deps:
                    if d.dma_sem is not None:
                        key = ("d", d.dma_sem)
                        sem = dsems[d.dma_sem]
                        val = d.dma_val
                    else:
                        key = ("e", d.eng)
                        sem = sems[d.eng]
                        val = d.sigval
                    if val > waited.get(key, 0) and val > need.get(key, (None, 0))[1]:
                        need[key] = (sem, val)
                for key, (sem, val) in need.items():
                    eng.wait_ge(sem, val)
                    waited[key] = val
                if o.fn is None:
                    continue
                inst = o.fn(eng)
                if o.dma_sem is not None:
                    inst.then_inc(dsems[o.dma_sem], 16)
                elif o.signal:
                    inst.then_inc(sems[e], 1)

        with nc.Block() as block:
            @block.sync
            def _(eng):
                run("sp", eng)

            @block.scalar
            def _(eng):
                run("act", eng)

            @block.vector
            def _(eng):
                run("dve", eng)

            @block.gpsimd
            def _(eng):
                run("pool", eng)

            @block.tensor
            def _(eng):
                run("pe", eng)


def build_nc():
    nc = bass.Bass("TRN2", target_bir_lowering=False)
    P = Prog(nc)

    def dram_in(name, shape):
        return nc.dram_tensor(name, list(shape), F32, kind="ExternalInput").ap()

    x_d = dram_in("x", [NSEQ * S, D])
    cT_d = dram_in("cT", [128, KC * NSEQ])
    wada_d = dram_in("w_ada", [D, 3 * D])
    badaT_d = dram_in("badaT", [128, 24])
    ngT_d = dram_in("ngT", [128, KC])
    win_d = dram_in("w_in", [D, D_IN])
    bf_d = dram_in("bf", [8, 1])
    wosb_d = dram_in("w_o_sb", [512, D])
    wofox_d = dram_in("w_o_fox", [512, D])
    bgT_d = dram_in("bgT", [128, 16])
    wout_d = dram_in("w_out", [D, D])
    fgT_d = dram_in("fgT", [128, KC])
    ident_d = dram_in("identf", [128, 128])
    maskA_d = dram_in("maskA", [128, 128])
    maskB_d = dram_in("maskB", [128, 128])
    out_d = nc.dram_tensor("out", [NSEQ * S, D], F32, kind="ExternalOutput").ap()

    base = int(nc.sbuf_base)
    base = (base + 63) // 64 * 64
    top = int(nc.sbuf_top)
    cur = [base]
    offs = {}

    def sb(name, shape, dtype, at=None):
        nbytes = int(np.prod(shape[1:])) * (4 if dtype == F32 else 2)
        nbytes = (nbytes + 63) // 64 * 64
        if at is None:
            off = cur[0]
            cur[0] += nbytes
        else:
            off = at
        assert off + nbytes <= top, (name, off, nbytes, top)
        offs[name] = off
        return nc.alloc_sbuf_tensor_at(name, list(shape), dtype, offset=off), off + nbytes

    def sbp(name, shape, dtype):
        return sb(name, shape, dtype)[0]

    identf = sbp("identf", [128, 128], F32)
    onesf = sbp("onesf", [128, 128], F32)
    ident = sbp("ident", [128, 128], BF16)
    maskA = sbp("maskA", [128, 128], BF16)
    maskB = sbp("maskB", [128, 128], BF16)
    mstage = sbp("mstage", [128, 128], F32)
    cT = sbp("cT", [128, KC, NSEQ], F32)
    cTb = sbp("cTb", [128, KC, NSEQ], BF16)
    badaT = sbp("badaT", [128, 24], F32)
    ngT = sbp("ngT", [128, KC], F32)
    bgT = sbp("bgT", [128, 16], F32)
    fgT = sbp("fgT", [128, KC], F32)
    bfc = sbp("bfc", [8, 1], F32)
    modT = sbp("modT", [128, 24, NSEQ], F32)
    acoef = sbp("acoef", [128, KC, NSEQ], F32)
    eps_t = sbp("eps_t", [128, 1], F32)
    ones_c = sbp("ones_c", [128, 1], F32)
    negb = sbp("negb", [8, 1], F32)
    stat = sbp("stat", [128, 8], F32)
    diag = [sbp("diag%d" % i, [128, 128], F32) for i in range(2)]
    gate_bc = sbp("gate_bc", [128, D], F32)
    fg_bc = sbp("fg_bc", [128, D], F32)
    wosb = sbp("wosb", [128, 4, D], BF16)
    wofox = sbp("wofox", [128, 4, D], BF16)
    wout = sbp("wout", [128, KC, D], BF16)
    hT = sbp("hT", [128, KC, S], BF16)
    ygA = sbp("ygA", [128, 4, S], BF16)
    ygB = sbp("ygB", [128, 4, S], BF16)
    NSTG = 2
    stg = [sbp("stg%d" % i, [128, KC, 128], F32) for i in range(NSTG)]
    xt = [sbp("xt%d" % i, [128, D], F32) for i in range(2)]
    arena0 = cur[0]
    xn = [sbp("xn%d" % i, [128, D], BF16) for i in range(2)]
    wq = sbp("wq", [128, KC, 128], BF16)
    wk = sbp("wk", [128, KC, 128], BF16)
    wv = sbp("wv", [128, KC, 128], BF16)
    wz = sbp("wz", [128, KC, 128], BF16)
    wfw = sbp("wfw", [128, KC, 8], BF16)
    fsp = sbp("fsp", [128, S], BF16)
    QB = sbp("QB", [128, 2, S], BF16)
    KB = sbp("KB", [128, 2, S], BF16)
    zT = sbp("zT", [128, S], BF16)
    VV = sbp("VV", [128, NB, 2, 128], BF16)
    partA_end = cur[0]
    partB0 = cur[0]
    zT2 = sbp("zT2", [128, S], BF16)
    zTa = [zT, zT2]
    gbuf = [sbp("gbuf%d" % i, [128, 1024], F32) for i in range(2)]
    bbuf = [sbp("bbuf%d" % i, [128, 1024], BF16) for i in range(3)]
    Pbuf = [sbp("Pbuf%d" % i, [128, 1026], BF16) for i in range(4)]
    carry = sbp("carry", [128, 1], F32)
    wbuf = [sbp("wbuf%d" % i, [128, 1024], BF16) for i in range(3)]
    wTb = [sbp("wTb%d" % i, [128, 1024], BF16) for i in range(3)]
    zsig = [sbp("zsig%d" % i, [128, 512], BF16) for i in range(1)] * 2
    attn_end = cur[0]
    xt4 = xt + [nc.alloc_sbuf_tensor_at("xtx%d" % i, [128, D], F32, offset=offs["ygA"] + 4096 * i) for i in range(2)]
    stgx = [nc.alloc_sbuf_tensor_at("stgx%d" % i, [128, KC, 128], F32, offset=offs["hT"] + 4096 * i) for i in range(8)]
    YG_TOKS = [("ygA", tg) for tg in range(NTG)]
    VVa = [nc.alloc_sbuf_tensor_at("VVa%d" % i, [128, NB, 2, 64], BF16, offset=offs["xt%d" % i]) for i in range(2)]
    fmid = nc.alloc_sbuf_tensor_at("fmid", [8, 512], BF16, offset=offs["wTb0"])
    ftmp = [nc.alloc_sbuf_tensor_at("ftmp%d" % i, [8, 512], F32, offset=o_)
            for i, o_ in enumerate([offs["gbuf1"], offs["gbuf1"] + 2048, offs["wbuf0"], offs["wbuf1"]])]
    FT = [("g", 1), ("w", 0), ("w", 1), ("wT", 0)]
    pbuf = [nc.alloc_sbuf_tensor_at("pbuf%d" % i, [128, 512], BF16, offset=offs["xt0"] + 1024 * i) for i in range(4)]
    recb = nc.alloc_sbuf_tensor_at("recb", [128, 512], F32, offset=offs["xt1"])
    tmpb = nc.alloc_sbuf_tensor_at("tmpb", [128, 512], F32, offset=offs["xt1"] + 2048)
    wgp = [nc.alloc_sbuf_tensor_at("wgp%d" % i, [128, KC, 128], BF16, offset=partB0 + 2048 * i) for i in range(16)]
    wg_end = partB0 + 2048 * 16
    PARTB_TOKS = ([("za", 1, tg) for tg in range(NTG)] + [("g", 0), ("g", 1), ("b", 0), ("b", 1), ("b", 2), ("P", 0), ("P", 1), ("P", 2), ("P", 3),
                  ("Pc", 2), ("Pc2", 2), ("w", 0), ("w", 1), ("w", 2), ("wT", 0), ("wT", 1), ("wT", 2), "Pone", ("zsig", 0), ("zsig", 1)])
    cur[0] = arena0
    G0s = [sbp("G0s%d" % i, [128, 512], BF16) for i in range(2)]
    G1s = [sbp("G1s%d" % i, [128, 512], BF16) for i in range(2)]
    t0b = [sbp("t0b%d" % i, [128, 512], F32) for i in range(2)]
    t1b = [sbp("t1b%d" % i, [128, 512], F32) for i in range(2)]
    mT = [sbp("mT%d" % i, [128, KC, 512], BF16) for i in range(2)]
    xr = [sbp("xr%d" % i, [128, D], F32) for i in range(2)]
    obuf = [sbp("obuf%d" % i, [128, D], F32) for i in range(2)]
    s5_end = cur[0]
    assert s5_end <= partA_end, (s5_end, partA_end)
    assert max(attn_end, wg_end) <= top, (attn_end, wg_end, top)
    print("SBUF map: arena0=%d partA_end=%d attn_end=%d wg_end=%d s5_end=%d top=%d" % (arena0, partA_end, attn_end, wg_end, s5_end, top))

    TB = [nc.alloc_psum_tensor("tb%d" % i, [128, 1024], BF16) for i in range(2)]
    ZP = [nc.alloc_psum_tensor("zp%d" % i, [128, 1024], F32) for i in range(2)]
    PB45 = [nc.alloc_psum_tensor("pb%d" % i, [128, 512], F32) for i in (4, 5)]
    PB = [ZP[0][:, 0:512], ZP[0][:, 512:1024], ZP[1][:, 0:512], ZP[1][:, 512:1024], PB45[0], PB45[1]]
    TBF = [TB[0][:, :].bitcast(F32), TB[1][:, :].bitcast(F32)]
    sqs = TB[1]
    sqs5 = TB[1]

    def dma_in(key, out_ap, in_ap, reads=(), writes=(), eng="sp"):
        return P.op(eng, lambda e: e.dma_start(out=out_ap, in_=in_ap), reads=reads, writes=writes, dma_key=key)

    def load_const(dst, src, name):
        dma_in("c_" + name, dst[:], src, writes=[name])

    load_const(identf, ident_d[:, :], "identf")
    load_const(cT, cT_d.rearrange("p (k b) -> p k b", b=NSEQ), "cT")
    load_const(badaT, badaT_d[:, :], "badaT")
    load_const(ngT, ngT_d[:, :], "ngT")
    load_const(bgT, bgT_d[:, :], "bgT")
    load_const(fgT, fgT_d[:, :], "fgT")
    load_const(bfc, bf_d[:, :], "bfc")
    P.op("pool", lambda e: e.tensor_copy(out=ident[:], in_=identf[:]), reads=["identf"], writes=["ident"])
    dma_in("c_mask", mstage[:], maskA_d[:, :], writes=["mstage"])
    P.op("pool", lambda e: e.tensor_copy(out=maskA[:], in_=mstage[:]), reads=["mstage"], writes=["maskA"])
    dma_in("c_mask", mstage[:], maskB_d[:, :], writes=["mstage"])
    P.op("pool", lambda e: e.tensor_copy(out=maskB[:], in_=mstage[:]), reads=["mstage"], writes=["maskB"])
    P.op("pool", lambda e: e.memset(onesf[:], 1.0), writes=["onesf"])
    P.op("pool", lambda e: e.memset(eps_t[:], EPS), writes=["eps"])
    P.op("pool", lambda e: e.memset(ones_c[:], 1.0), writes=["ones_c"])
    P.op("dve", lambda e: e.tensor_scalar(out=negb[:], in0=bfc[:], scalar1=-1.0, scalar2=None, op0=ALU.mult), reads=["bfc"], writes=["negb"])
    P.op("pool", lambda e: e.memset(Pbuf[0][:, 1025:1026], 1.0), writes=["Pone"])
    P.op("pool", lambda e: e.memset(Pbuf[1][:, 1025:1026], 1.0), writes=["Pone"])
    P.op("pool", lambda e: e.memset(Pbuf[2][:, 1025:1026], 1.0), writes=["Pone"])

    stg_i = [0]

    def load_piece(src_ap, kcn, ncols, dst_ap, dst_tokens, cast_eng="pool"):
        i = stg_i[0] % NSTG
        stg_i[0] += 1
        st = stg[i]
        dma_in("stg%d" % i, st[:, 0:kcn, 0:ncols], src_ap, writes=[("stg", i)])
        P.op(cast_eng, lambda e: e.tensor_copy(out=dst_ap, in_=st[:, 0:kcn, 0:ncols]),
             reads=[("stg", i)], writes=dst_tokens)

    def staged_items(pieces, cast_eng):
        slots = {}

        def dma_fn(i):
            src_ap, kcn, ncols, dst_ap, toks = pieces[i]
            k = stg_i[0] % NSTG
            stg_i[0] += 1
            slots[i] = k
            dma_in("stg%d" % k, stg[k][:, 0:kcn, 0:ncols], src_ap, writes=[("stg", k)])

        def cast_fn(i):
            src_ap, kcn, ncols, dst_ap, toks = pieces[i]
            k = slots[i]
            st = stg[k]
            if cast_eng == "act":
                P.op("act", lambda e: e.activation(out=dst_ap, in_=st[:, 0:kcn, 0:ncols], func=AF.Copy),
                     reads=[("stg", k)], writes=toks)
            else:
                P.op(cast_eng, lambda e: e.tensor_copy(out=dst_ap, in_=st[:, 0:kcn, 0:ncols]),
                     reads=[("stg", k)], writes=toks)
        n = len(pieces)
        items = [lambda: [dma_fn(i) for i in range(min(NSTG, n))]]
        for i in range(n):
            def it(i=i):
                cast_fn(i)
                if i + NSTG < n:
                    dma_fn(i + NSTG)
            items.append(it)
        return items

    def chunk_pieces(offq, offk, offv, offz):
        return [(win_v[:, :, off:off + 128], KC, 128, dst[:, :, :], [nm])
                for nm, dst, off in (("wq", wq, offq), ("wk", wk, offk), ("wv", wv, offv), ("wz", wz, offz))]

    win_v = win_d.rearrange("(k p) n -> p k n", p=128)
    wada_v = wada_d.rearrange("(k p) n -> p k n", p=128)
    wosb_v = wosb_d.rearrange("(k p) n -> p k n", p=128)
    wofox_v = wofox_d.rearrange("(k p) n -> p k n", p=128)
    wout_v = wout_d.rearrange("(k p) n -> p k n", p=128)

    def bcast_rows(col_fn, dst, dst_tok, src_tok):
        for half in range(2):
            bank = PB[1]
            for q in range(4):
                kc = half * 4 + q
                dg = diag[kc % 2]
                P.op("dve", lambda e, dg=dg, kc=kc: e.tensor_scalar(out=dg[:], in0=identf[:], scalar1=col_fn(kc),
                                                                     scalar2=None, op0=ALU.mult),
                     reads=["identf", src_tok], writes=[("diag", kc % 2)])
                P.op("pe", lambda e, dg=dg, q=q: e.matmul(bank[:, q * 128:(q + 1) * 128], lhsT=onesf[:], rhs=dg[:],
                                                           start=True, stop=True),
                     reads=["onesf", ("diag", kc % 2)], writes=[("ps", 1)])
            P.op("act", lambda e, half=half: e.activation(out=dst[:, half * 512:(half + 1) * 512], in_=bank[:, :], func=AF.Copy),
                 reads=[("ps", 1)], writes=[dst_tok])

    bcast_rows(lambda kc: fgT[:, kc:kc + 1], fg_bc, "fg_bc", "fgT")

    modps = PB[0]
    P.op("dve", lambda e: e.tensor_copy(out=cTb[:], in_=cT[:]), reads=["cT"], writes=["cTb"])
    def mod_dma(j):
        i = j % 8
        dma_in("stgx%d" % i, stgx[i][:, :, :], wada_v[:, :, j * 128:(j + 1) * 128], writes=[("stgx", i)])

    for j in range(8):
        mod_dma(j)
    for j in range(24):
        i = j % 8
        st = stgx[i]
        wab = xn[j % 2][:, :].rearrange("p (k n) -> p k n", n=128)
        P.op("dve", lambda e, st=st, wab=wab: e.tensor_copy(out=wab, in_=st[:, :, :]), reads=[("stgx", i)], writes=[("xn", j % 2)])

        def mm(e, wab=wab, j=j):
            r = None
            for kc in range(KC):
                r = e.matmul(modps[:, 2 * j:2 * j + 2], lhsT=wab[:, kc, :], rhs=cTb[:, kc, :],
                             start=(kc == 0), stop=(kc == KC - 1))
            return r
        P.op("pe", mm, reads=[("xn", j % 2), "cTb"], writes=[("ps", 0)])
        if j + 8 < 24:
            mod_dma(j + 8)
    modps_v = modps[:, 0:48].rearrange("p (j b) -> p j b", b=NSEQ)
    for b in range(NSEQ):
        P.op("dve", lambda e, b=b: e.tensor_tensor(out=modT[:, :, b], in0=modps_v[:, :, b], in1=badaT[:, :], op=ALU.add),
             reads=[("ps", 0), "badaT"], writes=["modT"])
        P.op("dve", lambda e, b=b: e.scalar_tensor_tensor(out=acoef[:, :, b], in0=modT[:, 8:16, b], scalar=1.0,
                                                           in1=ngT[:, :], op0=ALU.add, op1=ALU.mult),
             reads=["modT", "ngT"], writes=["acoef"])

    def w5_pieces(cbs):
        pcs = []
        for cb in cbs:
            pcs.append((wosb_v[:, :, cb * 128:(cb + 1) * 128], 4, 128, wosb[:, :, cb * 128:(cb + 1) * 128], [("wosb", cb)]))
            pcs.append((wofox_v[:, :, cb * 128:(cb + 1) * 128], 4, 128, wofox[:, :, cb * 128:(cb + 1) * 128], [("wofox", cb)]))
            pcs.append((wout_v[:, :, cb * 128:(cb + 1) * 128], 8, 128, wout[:, :, cb * 128:(cb + 1) * 128], [("wout", cb)]))
        return pcs


    out_i = [0]

    for sq in range(NSEQ):
        row0 = sq * S
        P.op("pool", lambda e: e.memset(VV[:, :, 0, 64:128], 1.0), writes=["VVones"])
        P.op("pool", lambda e: e.memset(VV[:, :, 1, 0:64], 1.0), writes=["VVones"])
        P.op("pool", lambda e: e.memset(Pbuf[0][:, 1025:1026], 1.0), writes=["Pone"])
        P.op("pool", lambda e: e.memset(Pbuf[1][:, 1025:1026], 1.0), writes=["Pone"])
        P.op("pool", lambda e: e.memset(Pbuf[2][:, 1025:1026], 1.0), writes=["Pone"])
        i = stg_i[0] % NSTG
        stg_i[0] += 1
        dma_in("stg%d" % i, stg[i][:, :, 0:8], win_v[:, :, OFF_F:OFF_F + 8], writes=[("stg", i)])
        P.op("pool", lambda e, i=i: e.tensor_copy(out=wfw[:], in_=stg[i][:, :, 0:8]), reads=[("stg", i)], writes=["wfw"])

        def hT_toks(tg):
            r = []
            for tb in range(4 * tg, 4 * tg + 4):
                r += [("hT", tb)]
            return r

        def x_dma(tb):
            xi = tb % 4
            dma_in("xt%d" % xi, xt4[xi][:], x_d[row0 + tb * 128: row0 + (tb + 1) * 128, :], writes=[("xt", xi)])

        def norm0(tb, sq=sq):
            par = tb % 2
            xi = tb % 4
            xb = xt4[xi]
            xnb = xn[par]
            s0 = 3 * par
            xextra = YG_TOKS if xi >= 2 else []
            P.op("act", lambda e: e.activation(out=gbuf[0][:], in_=xb[:], func=AF.Square, accum_out=stat[:, s0:s0 + 1]),
                 reads=[("xt", xi)] + xextra, writes=[("g", 0), ("ss", par)])
            P.op("act", lambda e: e.activation(out=stat[:, s0 + 1:s0 + 2], in_=stat[:, s0:s0 + 1], func=AF.Ln, scale=1.0 / D, bias=eps_t[:, 0:1]),
                 reads=[("ss", par), "eps"], writes=[("sd", par)])
            P.op("act", lambda e: e.activation(out=stat[:, s0 + 2:s0 + 3], in_=stat[:, s0 + 1:s0 + 2], func=AF.Exp, scale=-0.5),
                 reads=[("sd", par)], writes=[("rstd", par)])
            P.op("dve", lambda e: e.tensor_scalar(out=xnb[:], in0=xb[:], scalar1=stat[:, s0 + 2:s0 + 3], scalar2=None, op0=ALU.mult),
                 reads=[("xt", xi), ("rstd", par)] + xextra, writes=[("xn", par)])
            if tb + 4 < NB:
                x_dma(tb + 4)

        def norm1(tb, sq=sq):
            par = tb % 2
            xnb = xn[par]
            tv = TB[par][:, :].rearrange("p (k t) -> p k t", t=128)

            def tr(e):
                r = None
                for kc in range(KC):
                    r = e.transpose(out=tv[:, kc, :], in_=xnb[:, kc * 128:(kc + 1) * 128], identity=ident[:])
                return r
            P.op("pe", tr, reads=[("xn", par), "ident"], writes=[("tb", par)])
            if False:
                def ev(e):
                    r = None
                    for kc in range(KC):
                        r = e.activation(out=hT[:, kc, tb * 128:(tb + 1) * 128], in_=tv[:, kc, :], func=AF.Identity,
                                         scale=acoef[:, kc, sq:sq + 1], bias=modT[:, kc, sq:sq + 1])
                    return r
                P.op("act", ev, reads=[("tb", par), "acoef", "modT"], writes=[("hT", tb)])
            else:
                def ev(e):
                    r = None
                    for kc in range(KC):
                        r = e.tensor_scalar(out=hT[:, kc, tb * 128:(tb + 1) * 128], in0=tv[:, kc, :],
                                            scalar1=acoef[:, kc, sq:sq + 1], scalar2=modT[:, kc, sq:sq + 1],
                                            op0=ALU.mult, op1=ALU.add)
                    return r
                P.op("dve", ev, reads=[("tb", par), "acoef", "modT"], writes=[("hT", tb)])

        def f_phase(tg):
            cols = slice(tg * 512, (tg + 1) * 512)
            fbank = PB[1]

            def fmm(e, cols=cols):
                r = None
                for kc in range(KC):
                    r = e.matmul(fbank[0:8, :], lhsT=wfw[:, kc, :], rhs=hT[:, kc, cols], start=(kc == 0), stop=(kc == KC - 1))
                return r
            P.op("pe", fmm, reads=["wfw"] + hT_toks(tg), writes=[("ps", 1)])
            P.op("act", lambda e: e.activation(out=ftmp[0][:], in_=fbank[0:8, :], func=AF.Exp, bias=negb[:, 0:1], scale=-1.0),
                 reads=[("ps", 1), "negb"] + FT, writes=["f0"])
            P.op("act", lambda e: e.activation(out=ftmp[0][:], in_=ftmp[0][:], func=AF.Ln, bias=ones_c[0:8, 0:1], scale=1.0),
                 reads=["f0", "ones_c"] + FT, writes=["f0"])
            Fc = ftmp[1 + tg % 2]
            Fp = ftmp[1 + (tg + 1) % 2]
            init = 0.0 if tg == 0 else Fp[:, 511:512]
            P.op("dve", lambda e, Fc=Fc, init=init: e.tensor_tensor_scan(out=Fc[:], data0=ftmp[0][:], data1=ftmp[0][:], initial=init,
                                                                          op0=ALU.add, op1=ALU.max),
                 reads=["f0", ("F", (tg + 1) % 2)] + FT, writes=[("F", tg % 2)])
            P.op("dve", lambda e, Fc=Fc, cols=cols: e.tensor_scalar(out=fsp[0:8, cols], in0=Fc[:], scalar1=-8.0, scalar2=None, op0=ALU.mult),
                 reads=[("F", tg % 2)], writes=[("fsp", tg)])
            P.op("dve", lambda e, Fc=Fc, cols=cols: e.scalar_tensor_tensor(out=ftmp[3][:], in0=Fc[:], scalar=-8.0, in1=fsp[0:8, cols],
                                                                            op0=ALU.mult, op1=ALU.subtract),
                 reads=[("F", tg % 2), ("fsp", tg)] + FT, writes=["r1"])
            P.op("dve", lambda e: e.tensor_copy(out=fmid[:], in_=ftmp[3][:]), reads=["r1"] + FT, writes=["fmid"])
            P.op("dve", lambda e, cols=cols: e.tensor_copy(out=fsp[32:40, cols], in_=fmid[:]), reads=["fmid"], writes=[("fsp1", tg)])
            P.op("dve", lambda e, cols=cols: e.tensor_tensor(out=fsp[64:72, cols], in0=ftmp[3][:], in1=fmid[:], op=ALU.subtract),
                 reads=["r1", "fmid"], writes=[("fsp2", tg)])
        fsp_toks = []
        for tg in range(NTG):
            fsp_toks += [("fsp", tg), ("fsp1", tg), ("fsp2", tg)]

        def load_chunk_weights(offs):
            for nm, (dst, off) in offs.items():
                load_piece(win_v[:, :, off:off + 128], KC, 128, dst[:, :, :], [nm])

        def proj_fm(wt, wtok, tg, bank_i):
            bank = PB[bank_i] if bank_i < 10 else TBF[bank_i - 10]
            cols = slice(tg * 512, (tg + 1) * 512)

            def mm(e):
                r = None
                for kc in range(KC):
                    r = e.matmul(bank[:, :], lhsT=wt[:, kc, :], rhs=hT[:, kc, cols], start=(kc == 0), stop=(kc == KC - 1))
                return r
            P.op("pe", mm, reads=[wtok] + hT_toks(tg), writes=[("ps", bank_i) if bank_i < 10 else ("tb", bank_i - 10)])
            return bank

        def proj_v(c):
            for g4 in range(NB // 4):
                bank = TBF[g4 % 2]

                def mm(e, g4=g4, bank=bank):
                    r = None
                    for q in range(4):
                        sbk = g4 * 4 + q
                        for kc in range(KC):
                            r = e.matmul(bank[:, q * 128:(q + 1) * 128], lhsT=hT[:, kc, sbk * 128:(sbk + 1) * 128], rhs=wv[:, kc, :],
                                         start=(kc == 0), stop=(kc == KC - 1))
                    return r
                P.op("pe", mm, reads=["wv"] + hT_toks(g4), writes=[("tb", g4 % 2)])
                bv = bank[:, :].rearrange("p (q n) -> p q n", n=128)

                def ev(e, g4=g4, bv=bv):
                    e.activation(out=VV[:, g4 * 4:(g4 + 1) * 4, 0, 0:64], in_=bv[:, :, 0:64], func=AF.Copy)
                    return e.activation(out=VV[:, g4 * 4:(g4 + 1) * 4, 1, 64:128], in_=bv[:, :, 64:128], func=AF.Copy)
                P.op("act", ev, reads=[("tb", g4 % 2), "VVones"], writes=[("VV", g4)])

        VV_toks = [("VV", g) for g in range(NB // 4)]

        rowc = [0]

        deferred = []

        def flush_deferred():
            while deferred:
                deferred.pop(0)()

        def a_items(c):
            par = c % 2
            qTp = QB[:, par, :]
            kTp = KB[:, par, :]
            zTp = zTa[par]
            Vp = VVa[par]
            items = staged_items(chunk_pieces(OFF_QA + c * 128, OFF_KA + c * 128, OFF_VA + c * 128, OFF_ZA + c * 128), "act" if c == 0 else "dve")

            def mk(wt, wtok, tg, dst, tokname, func):
                def it():
                    cols = slice(tg * 512, (tg + 1) * 512)
                    bk = proj_fm(wt, wtok, tg, 5)

                    def ev():
                        if func is None:
                            zs = zsig[tg % 2]
                            P.op("act", lambda e: e.activation(out=zs[:], in_=bk[:, :], func=AF.Sigmoid),
                                 reads=[("ps", 5)], writes=[("zsig", 0)])
                            P.op("dve", lambda e: e.tensor_tensor(out=dst[:, cols], in0=bk[:, :], in1=zs[:], op=ALU.mult),
                                 reads=[("ps", 5), ("zsig", 0)], writes=[(tokname, par, tg)])
                        else:
                            P.op("act", lambda e: e.activation(out=dst[:, cols], in_=bk[:, :], func=func),
                                 reads=[("ps", 5)], writes=[(tokname, par, tg)])
                    deferred.append(ev)
                return it
            for tg in range(NTG):
                items.append(mk(wq, "wq", tg, qTp, "qa", AF.Copy))
                items.append(mk(wk, "wk", tg, kTp, "ka", AF.Copy))
                items.append(mk(wz, "wz", tg, zTp, "za", None))

            def mkv(g4):
                def it():
                    bank = PB[5]

                    def mm(e):
                        r = None
                        for q in range(4):
                            sbk = g4 * 4 + q
                            for kc in range(KC):
                                r = e.matmul(bank[:, q * 128:(q + 1) * 128], lhsT=hT[:, kc, sbk * 128:(sbk + 1) * 128], rhs=wv[:, kc, :],
                                             start=(kc == 0), stop=(kc == KC - 1))
                        return r
                    P.op("pe", mm, reads=["wv"] + hT_toks(g4), writes=[("ps", 5)])
                    deferred.append(lambda: P.op("act", lambda e: e.activation(
                        out=Vp[:, g4 * 4:(g4 + 1) * 4, :, :], in_=bank[:, :].rearrange("p (q h n) -> p q h n", h=2, n=64), func=AF.Copy),
                        reads=[("ps", 5)], writes=[("va", par, g4), ("xt", par)]))
                return it
            for g4 in range(NB // 4):
                items.append(mkv(g4))
            return items

        items0 = a_items(0)
        for tb in range(4):
            x_dma(tb)
        items0[0]()
        for step in range(NB + 1):
            if step < NB:
                norm0(step)
            if step >= 1:
                norm1(step - 1)
            if 1 <= step <= 4:
                items0[step]()
            if step >= 1 and step % 4 == 0:
                tg = step // 4 - 1
                f_phase(tg)
                for it in items0[5 + 3 * tg: 8 + 3 * tg]:
                    it()
                    flush_deferred()
            if step in (13, 14, 15):
                items0[17 + (step - 13)]()
                flush_deferred()
            if step == 9:
                bcast_rows(lambda kc, sq=sq: modT[:, 16 + kc, sq:sq + 1], gate_bc, "gate_bc", "modT")
        for it in items0[20:]:
            it()
            flush_deferred()
        def b0_items():
            items = []

            def mkz(tg):
                def it():
                    cols = slice(tg * 512, (tg + 1) * 512)
                    bk = proj_fm(wz, "wz", tg, 5)
                    zs = zsig[tg % 2]

                    def ev():
                        P.op("act", lambda e: e.activation(out=zs[:], in_=bk[:, :], func=AF.Sigmoid),
                             reads=[("ps", 5)], writes=[("zsig", 0)])
                        P.op("dve", lambda e: e.tensor_tensor(out=zT[:, cols], in0=bk[:, :], in1=zs[:], op=ALU.mult),
                             reads=[("ps", 5), ("zsig", 0)], writes=[("z", tg), ("za", 0, tg)])
                    deferred.append(ev)
                return it

            def mkv(g4):
                def it():
                    bank = PB[5]

                    def mm(e):
                        r = None
                        for q in range(4):
                            sbk = g4 * 4 + q
                            for kc in range(KC):
                                r = e.matmul(bank[:, q * 128:(q + 1) * 128], lhsT=hT[:, kc, sbk * 128:(sbk + 1) * 128], rhs=wv[:, kc, :],
                                             start=(kc == 0), stop=(kc == KC - 1))
                        return r
                    P.op("pe", mm, reads=["wv"] + hT_toks(g4), writes=[("ps", 5)])
                    bv = bank[:, :].rearrange("p (q n) -> p q n", n=128)

                    def ev(e):
                        e.activation(out=VV[:, g4 * 4:(g4 + 1) * 4, 0, 0:64], in_=bv[:, :, 0:64], func=AF.Copy)
                        return e.activation(out=VV[:, g4 * 4:(g4 + 1) * 4, 1, 64:128], in_=bv[:, :, 64:128], func=AF.Copy)
                    deferred.append(lambda: P.op("act", ev, reads=[("ps", 5), "VVones"], writes=[("VV", g4)]))
                return it
            for tg in range(NTG):
                items.append(mkz(tg))
            for g4 in range(NB // 4):
                items.append(mkv(g4))
            return items

        def run_chunk_A(c):
            par = c % 2
            qT = QB[:, par, :]
            kT = KB[:, par, :]
            zTc = zTa[par]
            Vc = VVa[par]
            bg = a_items(c + 1) if c < 3 else (staged_items(chunk_pieces(OFF_QB, OFF_KB, OFF_VB, OFF_ZB), "dve") + b0_items())
            if sq == 0:
                bg = bg + staged_items(w5_pieces([2 * c, 2 * c + 1]), "dve")
            k_toks = [("ka", par, tg) for tg in range(NTG)]
            Va_toks = [("va", par, g) for g in range(NB // 4)]
            rows = []
            for tg in range(NTG):
                for hh in range(2):
                    for qb in range(4 * tg, 4 * tg + 4):
                        ncols = (qb + 1) * 128
                        segs = []
                        hi = ncols
                        while hi > 0:
                            lo = max(0, hi - 1024)
                            segs.append((lo, hi))
                            hi = lo
                        pv_i = [0]
                        for si, (lo, hi) in enumerate(segs):
                            rows.append(dict(tg=tg, hh=hh, qb=qb, si=si, lo=lo, hi=hi, segs=segs, pv_i=pv_i,
                                             last=(hh == 1 and qb == 4 * tg + 3 and si == len(segs) - 1)))
            ybank = PB[4]
            p0rot = [0]
            p0of = {}

            def stA0(rw):
                r = rowc[0]
                rowc[0] += 1
                rw["r"] = r
                tg, hh, qb, si, lo, hi = rw["tg"], rw["hh"], rw["qb"], rw["si"], rw["lo"], rw["hi"]
                hb = 64 * hh
                n = hi - lo
                zb = r % 2
                zbanks = (PB[2 * zb], PB[2 * zb + 1])
                gb = gbuf[r % 2]
                ztoks = [("ps", 2 * zb), ("ps", 2 * zb + 1)]

                def qk(e):
                    rr = None
                    for p0 in range(0, n, 512):
                        w_ = min(512, n - p0)
                        bank = zbanks[p0 // 512]
                        isdiag = (si == 0 and p0 + w_ == n)
                        rr = e.matmul(bank[:, 0:w_], lhsT=qT[hb:hb + 64, qb * 128:(qb + 1) * 128],
                                      rhs=kT[hb:hb + 64, lo + p0: lo + p0 + w_], start=True, stop=not isdiag)
                        if isdiag:
                            rr = e.matmul(bank[:, w_ - 128:w_], lhsT=ident[:], rhs=maskA[:], start=False, stop=True)
                    return rr
                P.op("pe", qk, reads=[("qa", par, tg)] + k_toks + ["ident", "maskA"], writes=ztoks)

                def sg(e):
                    return e.activation(out=gb[:, 0:n], in_=ZP[zb][:, 0:n], func=AF.Sigmoid, scale=-0.125)
                P.op("act", sg, reads=ztoks, writes=[("g", r % 2)])
                bb = bbuf[r % 3]

                def sb_(e):
                    return e.activation(out=bb[:, 0:n], in_=ZP[zb][:, 0:n], func=AF.Sigmoid, scale=0.125)
                P.op("act", sb_, reads=ztoks, writes=[("b", r % 3)])

            def stA1a(rw):
                r, si, lo, hi, segs, qb = rw["r"], rw["si"], rw["lo"], rw["hi"], rw["segs"], rw["qb"]
                n = hi - lo
                gb = gbuf[r % 2]
                if si == 0:
                    pidx = p0rot[0] % 3
                    p0rot[0] += 1
                    p0of[qb] = pidx
                    init = 1.0
                    sc_reads = [("g", r % 2), "Pone"]
                else:
                    pprev = p0of[qb]
                    n0 = segs[0][1] - segs[0][0]
                    pidx = 3
                    P.op("dve", lambda e: e.tensor_copy(out=carry[:, 0:1], in_=Pbuf[pprev][:, 1025 - n0:1026 - n0]),
                         reads=[("P", pprev)], writes=[("Pc", 2)])
                    P.op("dve", lambda e: e.tensor_copy(out=Pbuf[3][:, 1025:1026], in_=carry[:, 0:1]),
                         reads=[("Pc", 2)], writes=[("Pc2", 2)])
                    init = carry[:, 0:1]
                    sc_reads = [("g", r % 2), ("Pc", 2)]
                rw["pidx"] = pidx
                Pb_ = Pbuf[pidx]
                P.op("dve", lambda e: e.tensor_tensor_scan(
                    out=Pb_[:, 1025 - n:1025][:, ::-1], data0=gb[:, 0:n][:, ::-1], data1=gb[:, 0:n][:, ::-1],
                    initial=init, op0=ALU.mult, op1=ALU.min),
                    reads=sc_reads, writes=[("P", pidx)])

            def stA1b(rw):
                r, lo, hi, pidx = rw["r"], rw["lo"], rw["hi"], rw["pidx"]
                n = hi - lo
                bb = bbuf[r % 3]
                wb = wbuf[r % 3]
                Pb_ = Pbuf[pidx]
                P.op("dve", lambda e: e.tensor_tensor(
                    out=wb[:, 0:n], in0=Pb_[:, 1026 - n:1026], in1=bb[:, 0:n], op=ALU.mult),
                    reads=[("P", pidx), ("Pc2", 2), "Pone", ("b", r % 3)], writes=[("w", r % 3)])

            def stA2(rw):
                r, lo, hi = rw["r"], rw["lo"], rw["hi"]
                n = hi - lo
                wb = wbuf[r % 3]
                tbk = TB[r % 2]
                wtb = wTb[r % 3]

                def trw(e):
                    rr = None
                    for j in range(n // 128):
                        rr = e.transpose(out=tbk[:, j * 128:(j + 1) * 128], in_=wb[:, j * 128:(j + 1) * 128], identity=ident[:])
                    return rr
                P.op("pe", trw, reads=[("w", r % 3), "ident"], writes=[("tb", r % 2)])
                P.op("act", lambda e: e.activation(out=wtb[:, 0:n], in_=tbk[:, 0:n], func=AF.Copy),
                     reads=[("tb", r % 2)], writes=[("wT", r % 3)])

            def stA3(rw, c=c):
                r, tg, hh, qb, lo, hi, pv_i = rw["r"], rw["tg"], rw["hh"], rw["qb"], rw["lo"], rw["hi"], rw["pv_i"]
                hb = 64 * hh
                n = hi - lo
                nkb = qb + 1
                wtb = wTb[r % 3]

                def pv(e):
                    rr = None
                    for j in range(n // 128):
                        kbk = lo // 128 + j
                        first = (pv_i[0] == 0)
                        pv_i[0] += 1
                        last = (pv_i[0] == nkb)
                        rr = e.matmul(ybank[hb:hb + 64, (qb % 4) * 128:(qb % 4 + 1) * 128],
                                      lhsT=Vc[:, kbk, hh, :], rhs=wtb[:, j * 128:(j + 1) * 128],
                                      start=first, stop=last)
                    return rr
                P.op("pe", pv, reads=[("wT", r % 3)] + Va_toks, writes=[("ps", 4)])
                if rw["last"]:
                    P.op("dve", lambda e: e.tensor_tensor(
                        out=ygA[:, c, tg * 512:(tg + 1) * 512], in0=ybank[:, :], in1=zTc[:, tg * 512:(tg + 1) * 512], op=ALU.mult),
                        reads=[("ps", 4), ("za", par, tg)], writes=[("ygA", tg)])

            return rows, (stA0, stA1a, stA1b, stA2, stA3), bg

        stream = []
        chunk_start = {}
        bgs = {}
        nrows = {}
        for c in range(4):
            rows_c, fns_c, bg_c = run_chunk_A(c)
            chunk_start[len(stream)] = c
            bgs[c] = bg_c
            nrows[c] = len(rows_c)
            stream += [(rw, fns_c) for rw in rows_c]
        nst = len(stream)
        cur_bg = []
        local0 = 0
        nr_c = 1
        for step in range(nst + 6):
            flush_deferred()
            if step < nst:
                if step in chunk_start:
                    while cur_bg:
                        cur_bg.pop(0)()
                        flush_deferred()
                    cur_bg = bgs[chunk_start[step]]
                    nr_c = nrows[chunk_start[step]]
                    local0 = step
                rw, fns = stream[step]
                fns[0](rw)
            for k, d in ((1, 1), (2, 2), (3, 4), (4, 6)):
                if 0 <= step - d < nst:
                    rw, fns = stream[step - d]
                    fns[k](rw)
            ls = step - local0
            if cur_bg and (ls == 0 or (ls >= 3 and (ls % 2 == 1 or 2 * len(cur_bg) > (nr_c + 3 - ls)))):
                cur_bg.pop(0)()
        flush_deferred()
        while cur_bg:
            cur_bg.pop(0)()
            flush_deferred()

        zrot = [0]
        yrot = [0]
        P.op("pool", lambda e: e.memset(QB[64:70, :, :], -1.0),
             writes=["QBaug"] + [("q", tg) for tg in range(NTG)] + [("qa", p_, tg) for p_ in range(2) for tg in range(NTG)])
        P.op("pool", lambda e: e.memset(KB[64:70, :, :], 1.0),
             writes=["KBaug"] + [("k", tg) for tg in range(NTG)] + [("ka", p_, tg) for p_ in range(2) for tg in range(NTG)])
        for c in range(4):
            for hh in range(2):
                for j in range(3):
                    hd = 2 * c + hh
                    dma_in("augq%d%d" % (hh, j), QB[64 + j:65 + j, hh, :], fsp[hd + 32 * j:hd + 32 * j + 1, :], reads=fsp_toks + ["QBaug"],
                           writes=[("qaug", hh, j)], eng="sp")
                    dma_in("augk%d%d" % (hh, j), KB[67 + j:68 + j, hh, :], fsp[hd + 32 * j:hd + 32 * j + 1, :], reads=fsp_toks + ["KBaug"],
                           writes=[("kaug", hh, j)], eng="sp")
            for tg in range(NTG):
                cols = slice(tg * 512, (tg + 1) * 512)
                bk = proj_fm(wq, "wq", tg, 10)

                def evq(e, bk=bk, cols=cols):
                    e.activation(out=QB[0:64, 0, cols], in_=bk[0:64, :], func=AF.Copy)
                    return e.activation(out=QB[0:64, 1, cols], in_=bk[64:128, :], func=AF.Copy)
                P.op("act", evq, reads=[("tb", 0)], writes=[("q", tg)])
                bk = proj_fm(wk, "wk", tg, 11)

                def evk(e, bk=bk, cols=cols):
                    e.activation(out=KB[0:64, 0, cols], in_=bk[0:64, :], func=AF.Copy)
                    return e.activation(out=KB[0:64, 1, cols], in_=bk[64:128, :], func=AF.Copy)
                P.op("act", evk, reads=[("tb", 1)], writes=[("k", tg)])
            for tg in range(NTG if c > 0 else 0):
                cols = slice(tg * 512, (tg + 1) * 512)
                bk = proj_fm(wz, "wz", tg, 10 + tg % 2)
                P.op("act", lambda e, bk=bk, cols=cols: e.activation(out=zT[:, cols], in_=bk[:, :], func=AF.Silu),
                     reads=[("tb", tg % 2)], writes=[("z", tg), ("za", 0, tg)])
            if c > 0:
                proj_v(c)
            pcs = []
            if c < 3:
                pcs += chunk_pieces(OFF_QB + (c + 1) * 128, OFF_KB + (c + 1) * 128, OFF_VB + (c + 1) * 128, OFF_ZB + (c + 1) * 128)
            for cb in range(4 * c, 4 * c + 4):
                pcs.append((win_v[:, :, OFF_G + cb * 128: OFF_G + (cb + 1) * 128], KC, 128, wgp[cb][:, :, :],
                            [("wg", cb)] + PARTB_TOKS))
            bgB = staged_items(pcs, "dve")
            aug_toks = [("qaug", hh, j) for hh in range(2) for j in range(3)] + [("kaug", hh, j) for hh in range(2) for j in range(3)]
            k_toks = [("k", tg) for tg in range(NTG)]
            rows = []
            for tg in range(NTG):
                for hh in range(2):
                    ybi = yrot[0] % 3
                    yrot[0] += 1
                    nkb = 4 * tg + 4
                    for kbk in range(nkb):
                        rows.append(dict(tg=tg, hh=hh, kbk=kbk, nkb=nkb, ybi=ybi))

            def stB0(rw):
                tg, hh, kbk = rw["tg"], rw["hh"], rw["kbk"]
                if kbk < 4 * tg:
                    c0 = 0
                    isdiag = False
                else:
                    c0 = (kbk - 4 * tg) * 128
                    isdiag = True
                rw["c0"] = c0
                zi = zrot[0] % 3
                pi = zrot[0] % 4
                zrot[0] += 1
                rw["pi"] = pi
                zbank = PB[zi]
                pb_ = pbuf[pi]

                def qk(e):
                    rr = e.matmul(zbank[:, c0:512], lhsT=KB[0:70, hh, kbk * 128:(kbk + 1) * 128],
                                  rhs=QB[0:70, hh, tg * 512 + c0:(tg + 1) * 512], start=True, stop=not isdiag)
                    if isdiag:
                        rr = e.matmul(zbank[:, c0:c0 + 128], lhsT=ident[:], rhs=maskB[:], start=False, stop=True)
                    return rr
                P.op("pe", qk, reads=[("q", tg)] + k_toks + aug_toks + ["ident", "maskB", "QBaug", "KBaug"], writes=[("ps", zi)])
                rw["zi"] = zi

            def stB0b(rw):
                c0, zi, pi = rw["c0"], rw["zi"], rw["pi"]
                zbank = PB[zi]
                pb_ = pbuf[pi]
                P.op("act", lambda e: e.activation(out=pb_[:, c0:512], in_=zbank[:, c0:512], func=AF.Exp, scale=0.125),
                     reads=[("ps", zi)], writes=[("p", pi)])

            def stB1(rw, c=c):
                tg, hh, kbk, nkb, ybi, c0, pi = rw["tg"], rw["hh"], rw["kbk"], rw["nkb"], rw["ybi"], rw["c0"], rw["pi"]
                ybank = PB[3 + ybi]
                pb_ = pbuf[pi]
                P.op("pe", lambda e: e.matmul(
                    ybank[:, c0:512], lhsT=VV[:, kbk, hh, :], rhs=pb_[:, c0:512], start=(kbk == 0), stop=(kbk == nkb - 1)),
                    reads=[("p", pi)] + VV_toks + ["VVones"], writes=[("ps", 3 + ybi)])
                if kbk == nkb - 1:
                    ys = slice(0, 64) if hh == 0 else slice(64, 128)
                    ds_ = slice(64, 128) if hh == 0 else slice(0, 64)
                    P.op("dve", lambda e: e.reciprocal(out=recb[ds_, :], in_=ybank[ds_, :]),
                         reads=[("ps", 3 + ybi)], writes=["rec"])
                    P.op("dve", lambda e: e.tensor_tensor(out=tmpb[ys, :], in0=ybank[ys, :], in1=recb[ds_, :], op=ALU.mult),
                         reads=[("ps", 3 + ybi), "rec"], writes=["tmpb"])
                    P.op("dve", lambda e: e.tensor_tensor(out=ygB[ys, c, tg * 512:(tg + 1) * 512], in0=tmpb[ys, :],
                                                          in1=zT[ys, tg * 512:(tg + 1) * 512], op=ALU.mult),
                         reads=["tmpb", ("z", tg)], writes=[("ygB", tg, hh)])

            nr = len(rows)
            for step in range(nr + 3):
                if step < nr:
                    stB0(rows[step])
                if 0 <= step - 1 < nr:
                    stB0b(rows[step - 1])
                if 0 <= step - 3 < nr:
                    stB1(rows[step - 3])
                if bgB and step % 4 == 1:
                    bgB.pop(0)()
            while bgB:
                bgB.pop(0)()

        P.barrier()
        def o_phase(tg, mt):
            mT_toks = [("mT", tg % 2, j) for j in range(8)]
            pending_tail = []
            for qq in range(4):
                tb = tg * 4 + qq
                oi = out_i[0] % 2
                out_i[0] += 1
                xb = xt[oi]
                xrb = xr[oi]
                ob = obuf[oi]
                dma_in("xt%d" % oi, xb[:], x_d[row0 + tb * 128: row0 + (tb + 1) * 128, :], writes=[("xt", oi)])
                for nh in range(2):
                    obank = PB[4 + nh]
                    ncol = slice(nh * 512, (nh + 1) * 512)

                    def mm_o(e, mt=mt, qq=qq, ncol=ncol, obank=obank):
                        rr = None
                        for kc in range(KC):
                            rr = e.matmul(obank[:, :], lhsT=mt[:, kc, qq * 128:(qq + 1) * 128], rhs=wout[:, kc, ncol], start=(kc == 0), stop=(kc == KC - 1))
                        return rr
                    P.op("pe", mm_o, reads=mT_toks + [("wout", cb_) for cb_ in range(8)], writes=[("ps", 4 + nh)])
                    P.op("dve", lambda e, obank=obank, ncol=ncol, xrb=xrb: e.tensor_tensor(out=xrb[:, ncol], in0=obank[:, :], in1=gate_bc[:, ncol], op=ALU.mult),
                         reads=[("ps", 4 + nh), "gate_bc"], writes=[("xr", oi, nh), ("xr2", oi, nh)])
                s5 = 3 * oi
                for nh in range(2):
                    ncol = slice(nh * 512, (nh + 1) * 512)
                    P.op("dve", lambda e, ncol=ncol, xrb=xrb, xb=xb: e.tensor_tensor(out=xrb[:, ncol], in0=xrb[:, ncol], in1=xb[:, ncol], op=ALU.add),
                         reads=[("xr", oi, nh), ("xt", oi)], writes=[("xr2", oi, nh)])
                P.op("act", lambda e, xrb=xrb, ob=ob, s5=s5: e.activation(out=ob[:], in_=xrb[:], func=AF.Square, accum_out=stat[:, s5:s5 + 1]),
                     reads=[("xr2", oi, 0), ("xr2", oi, 1)], writes=[("ob", oi), ("ss5", oi)])
                P.op("act", lambda e, s5=s5: e.activation(out=stat[:, s5 + 1:s5 + 2], in_=stat[:, s5:s5 + 1], func=AF.Ln, scale=1.0 / D, bias=eps_t[:, 0:1]),
                     reads=[("ss5", oi), "eps"], writes=[("sd5", oi)])
                P.op("act", lambda e, s5=s5: e.activation(out=stat[:, s5 + 2:s5 + 3], in_=stat[:, s5 + 1:s5 + 2], func=AF.Exp, scale=-0.5),
                     reads=[("sd5", oi)], writes=[("rstd5", oi)])

                while pending_tail:
                    pending_tail.pop(0)()

                def tail(oi=oi, xrb=xrb, ob=ob, tb=tb, s5=s5):
                    P.op("dve", lambda e: e.scalar_tensor_tensor(out=ob[:], in0=xrb[:], scalar=stat[:, s5 + 2:s5 + 3], in1=fg_bc[:],
                                                                 op0=ALU.mult, op1=ALU.mult),
                         reads=[("xr2", oi, 0), ("xr2", oi, 1), ("rstd5", oi), "fg_bc"], writes=[("ob", oi)])
                    dma_in("out%d" % oi, out_d[row0 + tb * 128: row0 + (tb + 1) * 128, :], ob[:], reads=[("ob", oi)], writes=[("outd", tb, sq)], eng="pool")
                pending_tail.append(tail)
            while pending_tail:
                pending_tail.pop(0)()
        for tg in range(NTG):
            cols = slice(tg * 512, (tg + 1) * 512)
            mt = mT[tg % 2]
            for j in range(8):
                jb = (tg * 8 + j) % 2
                fcols = slice(j * 128, (j + 1) * 128)

                def mm_pa(e, fcols=fcols, cols=cols):
                    rr = None
                    for kc in range(4):
                        rr = e.matmul(PB[0][:, :], lhsT=wosb[:, kc, fcols], rhs=ygA[:, kc, cols], start=(kc == 0), stop=(kc == 3))
                    return rr

                def mm_pb(e, fcols=fcols, cols=cols):
                    rr = None
                    for kc in range(4):
                        rr = e.matmul(PB[1][:, :], lhsT=wofox[:, kc, fcols], rhs=ygB[:, kc, cols], start=(kc == 0), stop=(kc == 3))
                    return rr

                def mm_g(e, j=j, cols=cols, which=0):
                    rr = None
                    for kc in range(KC):
                        rr = e.matmul(PB[2 + which][:, :], lhsT=wgp[which * 8 + j][:, kc, :], rhs=hT[:, kc, cols],
                                      start=(kc == 0), stop=(kc == KC - 1))
                    return rr
                P.op("pe", lambda e, f=mm_g: f(e, which=0), reads=[("wg", j)] + hT_toks(tg), writes=[("ps", 2)])
                P.op("act", lambda e, j=j, jb=jb: e.activation(out=G0s[jb][:], in_=PB[2][:, :], func=AF.Sigmoid, bias=bgT[:, j:j + 1], scale=1.0),
                     reads=[("ps", 2), "bgT"], writes=[("G0", jb)])
                P.op("pe", lambda e, f=mm_g: f(e, which=1), reads=[("wg", 8 + j)] + hT_toks(tg), writes=[("ps", 3)])
                P.op("act", lambda e, j=j, jb=jb: e.activation(out=G1s[jb][:], in_=PB[3][:, :], func=AF.Sigmoid, bias=bgT[:, 8 + j:9 + j], scale=1.0),
                     reads=[("ps", 3), "bgT"], writes=[("G1", jb)])
                P.op("pe", mm_pa, reads=[("wosb", j), ("ygA", tg)], writes=[("ps", 0)])
                P.op("pe", mm_pb, reads=[("wofox", j), ("ygB", tg, 0), ("ygB", tg, 1)], writes=[("ps", 1)])
                P.op("dve", lambda e, jb=jb: e.tensor_tensor(out=t0b[jb][:], in0=PB[0][:, :], in1=G0s[jb][:], op=ALU.mult),
                     reads=[("ps", 0), ("G0", jb)], writes=[("t0", jb)])
                P.op("dve", lambda e, jb=jb: e.tensor_tensor(out=t1b[jb][:], in0=PB[1][:, :], in1=G1s[jb][:], op=ALU.mult),
                     reads=[("ps", 1), ("G1", jb)], writes=[("t1", jb)])
                P.op("pool", lambda e, jb=jb, mt=mt, j=j: e.tensor_tensor(out=mt[:, j, :], in0=t0b[jb][:], in1=t1b[jb][:], op=ALU.add),
                     reads=[("t0", jb), ("t1", jb)], writes=[("mT", tg % 2, j)])
                if j == 1 and tg >= 1:
                    o_phase(tg - 1, mT[(tg - 1) % 2])
        o_phase(NTG - 1, mT[(NTG - 1) % 2])
        P.barrier()

    P.emit()
    return nc


def _build():
    return build_nc()


_NC_CACHE = {}


def _host_layout(inputs, core):
    b0 = core * NSEQ
    f32 = np.float32
    x = np.ascontiguousarray(inputs["x"][b0:b0 + NSEQ].reshape(NSEQ * S, D), dtype=f32)
    c = np.asarray(inputs["c"][b0:b0 + NSEQ], dtype=f32)
    cT = np.ascontiguousarray(c.reshape(NSEQ, KC, 128).transpose(2, 1, 0).reshape(128, KC * NSEQ))

    def fm(v, n):
        return np.ascontiguousarray(np.asarray(v, dtype=f32).reshape(n, 128).T)

    t = np.arange(128)
    maskA = np.where(t[None, :] >= t[:, None], MASKVAL, 0.0).astype(f32)
    maskB = np.where(t[:, None] > t[None, :], MASKVAL, 0.0).astype(f32)
    return {
        "x": x, "cT": cT,
        "w_ada": np.ascontiguousarray(inputs["w_ada"][0], dtype=f32),
        "badaT": fm(inputs["b_ada"][0], 24),
        "ngT": fm(inputs["norm_g"][0], KC),
        "w_in": np.ascontiguousarray(inputs["w_in"][0], dtype=f32),
        "bf": np.ascontiguousarray(np.asarray(inputs["b_forget"][0], dtype=f32).reshape(8, 1)),
        "w_o_sb": np.ascontiguousarray(inputs["w_o_sb"][0], dtype=f32),
        "w_o_fox": np.ascontiguousarray(inputs["w_o_fox"][0], dtype=f32),
        "bgT": fm(inputs["b_gate"][0], 16),
        "w_out": np.ascontiguousarray(inputs["w_out"][0], dtype=f32),
        "fgT": fm(inputs["final_g"], KC),
        "identf": np.eye(128, dtype=f32),
        "maskA": maskA, "maskB": maskB,
    }


def kernel(**inputs):
    inputs = {k: np.asarray(v) for k, v in inputs.items()}
    if "nc" not in _NC_CACHE:
        _NC_CACHE["nc"] = _build()
    nc = _NC_CACHE["nc"]
    in_maps = [_host_layout(inputs, i) for i in range(NCORES)]
    res = run_bass_kernel_spmd(nc, in_maps, core_ids=list(range(NCORES)))
    outs = [np.asarray(r["out"]).reshape(NSEQ, S, D) for r in res.results]
    return np.concatenate(outs, axis=0).astype(np.float32)
```

```python
import numpy as np
import concourse.bass as bass
import concourse.mybir as mybir
from concourse.bass_utils import run_bass_kernel_spmd

F32 = mybir.dt.float32
BF16 = mybir.dt.bfloat16
ALU = mybir.AluOpType
AF = mybir.ActivationFunctionType

NCORES = 8
S = 2048
D = 1024
NSEQ = 2
KC = 8
NB = S // 128
NTG = S // 512
EPS = 1e-6
OFF_QA, OFF_KA, OFF_VA, OFF_ZA = 0, 512, 1024, 1536
OFF_QB, OFF_KB, OFF_VB, OFF_ZB = 2048, 2560, 3072, 3584
OFF_F, OFF_G = 4096, 4104
D_IN = 6152
MASKVAL = -30000.0

ENGS = ["sp", "act", "dve", "pool", "pe"]


class Op:
    __slots__ = ("eng", "fn", "deps", "signal", "sigval", "dma_sem", "dma_val", "idx")

    def __init__(self, eng, fn):
        self.eng = eng
        self.fn = fn
        self.deps = []
        self.signal = False
        self.sigval = 0
        self.dma_sem = None
        self.dma_val = 0


class Prog:
    def __init__(self, nc):
        self.nc = nc
        self.ops = {e: [] for e in ENGS}
        self.last_writer = {}
        self.readers = {}
        self.dma_counts = {}
        self.all_dma_ops = []

    def op(self, eng, fn, reads=(), writes=(), dma_key=None):
        o = Op(eng, fn)
        deps = {}
        for t in reads:
            w = self.last_writer.get(t)
            if w is not None:
                deps[id(w)] = w
        for t in writes:
            w = self.last_writer.get(t)
            if w is not None:
                deps[id(w)] = w
            for r in self.readers.get(t, ()):
                deps[id(r)] = r
        for d in deps.values():
            if d is o:
                continue
            if d.dma_sem is None and d.eng == eng and eng in ("pe", "sp"):
                continue
            o.deps.append(d)
        for t in reads:
            self.readers.setdefault(t, []).append(o)
        for t in writes:
            self.last_writer[t] = o
            self.readers[t] = []
        if dma_key is not None:
            c = self.dma_counts.get(dma_key, 0) + 1
            self.dma_counts[dma_key] = c
            o.dma_sem = dma_key
            o.dma_val = 16 * c
            self.all_dma_ops.append(o)
        self.ops[eng].append(o)
        return o

    def barrier(self):
        lasts = []
        for e in ENGS:
            real = [o for o in self.ops[e] if o.fn is not None]
            if real:
                lasts.append(real[-1])
        dma_last = {}
        for o in self.all_dma_ops:
            dma_last[o.dma_sem] = o
        for e in ENGS:
            o = Op(e, None)
            for l in lasts:
                if l.eng != e or l.dma_sem is not None:
                    o.deps.append(l)
            for d in dma_last.values():
                o.deps.append(d)
            self.ops[e].append(o)

    def emit(self):
        nc = self.nc
        for e in ENGS:
            for o in self.ops[e]:
                for d in o.deps:
                    if d.dma_sem is None:
                        d.signal = True
        sems = {e: nc.alloc_semaphore("done_" + e) for e in ENGS}
        dsems = {k: nc.alloc_semaphore("dma_%s" % (str(k).replace(" ", "").replace("'", "").replace("(", "_").replace(")", "_").replace(",", "_")))
                 for k in self.dma_counts}
        for e in ENGS:
            c = 0
            for o in self.ops[e]:
                if o.signal:
                    c += 1
                    o.sigval = c
        prog = self

        def run(e, eng):
            waited = {}
            for o in prog.ops[e]:
                need = {}
                for d in o.deps:
                    if d.dma_sem is not None:
                        key = ("d", d.dma_sem)
                        sem = dsems[d.dma_sem]
                        val = d.dma_val
                    else:
                        key = ("e", d.eng)
                        sem = sems[d.eng]
                        val = d.sigval
                    if val > waited.get(key, 0) and val > need.get(key, (None, 0))[1]:
                        need[key] = (sem, val)
                for key, (sem, val) in need.items():
                    eng.wait_ge(sem, val)
                    waited[key] = val
                if o.fn is None:
                    continue
                inst = o.fn(eng)
                if o.dma_sem is not None:
                    inst.then_inc(dsems[o.dma_sem], 16)
                elif o.signal:
                    inst.then_inc(sems[e], 1)

        with nc.Block() as block:
            @block.sync
            def _(eng):
                run("sp", eng)

            @block.scalar
            def _(eng):
                run("act", eng)

            @block.vector
            def _(eng):
                run("dve", eng)

            @block.gpsimd
            def _(eng):
                run("pool", eng)

            @block.tensor
            def _(eng):
                run("pe", eng)


def build_nc():
    nc = bass.Bass("TRN2", target_bir_lowering=False)
    P = Prog(nc)

    def dram_in(name, shape):
        return nc.dram_tensor(name, list(shape), F32, kind="ExternalInput").ap()

    x_d = dram_in("x", [NSEQ * S, D])
    cT_d = dram_in("cT", [128, KC * NSEQ])
    wada_d = dram_in("w_ada", [D, 3 * D])
    badaT_d = dram_in("badaT", [128, 24])
    ngT_d = dram_in("ngT", [128, KC])
    win_d = dram_in("w_in", [D, D_IN])
    bf_d = dram_in("bf", [8, 1])
    wosb_d = dram_in("w_o_sb", [512, D])
    wofox_d = dram_in("w_o_fox", [512, D])
    bgT_d = dram_in("bgT", [128, 16])
    wout_d = dram_in("w_out", [D, D])
    fgT_d = dram_in("fgT", [128, KC])
    ident_d = dram_in("identf", [128, 128])
    maskA_d = dram_in("maskA", [128, 128])
    maskB_d = dram_in("maskB", [128, 128])
    out_d = nc.dram_tensor("out", [NSEQ * S, D], F32, kind="ExternalOutput").ap()

    base = int(nc.sbuf_base)
    base = (base + 63) // 64 * 64
    top = int(nc.sbuf_top)
    cur = [base]
    offs = {}

    def sb(name, shape, dtype, at=None):
        nbytes = int(np.prod(shape[1:])) * (4 if dtype == F32 else 2)
        nbytes = (nbytes + 63) // 64 * 64
        if at is None:
            off = cur[0]
            cur[0] += nbytes
        else:
            off = at
        assert off + nbytes <= top, (name, off, nbytes, top)
        offs[name] = off
        return nc.alloc_sbuf_tensor_at(name, list(shape), dtype, offset=off), off + nbytes

    def sbp(name, shape, dtype):
        return sb(name, shape, dtype)[0]

    identf = sbp("identf", [128, 128], F32)
    onesf = sbp("onesf", [128, 128], F32)
    ident = sbp("ident", [128, 128], BF16)
    maskA = sbp("maskA", [128, 128], BF16)
    maskB = sbp("maskB", [128, 128], BF16)
    mstage = sbp("mstage", [128, 128], F32)
    cT = sbp("cT", [128, KC, NSEQ], F32)
    cTb = sbp("cTb", [128, KC, NSEQ], BF16)
    badaT = sbp("badaT", [128, 24], F32)
    ngT = sbp("ngT", [128, KC], F32)
    bgT = sbp("bgT", [128, 16], F32)
    fgT = sbp("fgT", [128, KC], F32)
    bfc = sbp("bfc", [8, 1], F32)
    modT = sbp("modT", [128, 24, NSEQ], F32)
    acoef = sbp("acoef", [128, KC, NSEQ], F32)
    eps_t = sbp("eps_t", [128, 1], F32)
    ones_c = sbp("ones_c", [128, 1], F32)
    negb = sbp("negb", [8, 1], F32)
    stat = sbp("stat", [128, 8], F32)
    diag = [sbp("diag%d" % i, [128, 128], F32) for i in range(2)]
    gate_bc = sbp("gate_bc", [128, D], F32)
    fg_bc = sbp("fg_bc", [128, D], F32)
    wosb = sbp("wosb", [128, 4, D], BF16)
    wofox = sbp("wofox", [128, 4, D], BF16)
    wout = sbp("wout", [128, KC, D], BF16)
    hT = sbp("hT", [128, KC, S], BF16)
    ygA = sbp("ygA", [128, 4, S], BF16)
    ygB = sbp("ygB", [128, 4, S], BF16)
    NSTG = 2
    stg = [sbp("stg%d" % i, [128, KC, 128], F32) for i in range(NSTG)]
    xt = [sbp("xt%d" % i, [128, D], F32) for i in range(2)]
    arena0 = cur[0]
    xn = [sbp("xn%d" % i, [128, D], BF16) for i in range(2)]
    wq = sbp("wq", [128, KC, 128], BF16)
    wk = sbp("wk", [128, KC, 128], BF16)
    wv = sbp("wv", [128, KC, 128], BF16)
    wz = sbp("wz", [128, KC, 128], BF16)
    wfw = sbp("wfw", [128, KC, 8], BF16)
    fsp = sbp("fsp", [128, S], BF16)
    QB = sbp("QB", [128, 2, S], BF16)
    KB = sbp("KB", [128, 2, S], BF16)
    zT = sbp("zT", [128, S], BF16)
    VV = sbp("VV", [128, NB, 2, 128], BF16)
    partA_end = cur[0]
    partB0 = cur[0]
    zT2 = sbp("zT2", [128, S], BF16)
    zTa = [zT, zT2]
    gbuf = [sbp("gbuf%d" % i, [128, 1024], F32) for i in range(2)]
    bbuf = [sbp("bbuf%d" % i, [128, 1024], BF16) for i in range(3)]
    Pbuf = [sbp("Pbuf%d" % i, [128, 1026], BF16) for i in range(4)]
    carry = sbp("carry", [128, 1], F32)
    wbuf = [sbp("wbuf%d" % i, [128, 1024], BF16) for i in range(3)]
    wTb = [sbp("wTb%d" % i, [128, 1024], BF16) for i in range(3)]
    zsig = [sbp("zsig%d" % i, [128, 512], BF16) for i in range(1)] * 2
    attn_end = cur[0]
    xt4 = xt + [nc.alloc_sbuf_tensor_at("xtx%d" % i, [128, D], F32, offset=offs["ygA"] + 4096 * i) for i in range(2)]
    stgx = [nc.alloc_sbuf_tensor_at("stgx%d" % i, [128, KC, 128], F32, offset=offs["hT"] + 4096 * i) for i in range(8)]
    YG_TOKS = [("ygA", tg) for tg in range(NTG)]
    VVa = [nc.alloc_sbuf_tensor_at("VVa%d" % i, [128, NB, 2, 64], BF16, offset=offs["xt%d" % i]) for i in range(2)]
    fmid = nc.alloc_sbuf_tensor_at("fmid", [8, 512], BF16, offset=offs["wTb0"])
    ftmp = [nc.alloc_sbuf_tensor_at("ftmp%d" % i, [8, 512], F32, offset=o_)
            for i, o_ in enumerate([offs["gbuf1"], offs["gbuf1"] + 2048, offs["wbuf0"], offs["wbuf1"]])]
    FT = [("g", 1), ("w", 0), ("w", 1), ("wT", 0)]
    pbuf = [nc.alloc_sbuf_tensor_at("pbuf%d" % i, [128, 512], BF16, offset=offs["xt0"] + 1024 * i) for i in range(4)]
    recb = nc.alloc_sbuf_tensor_at("recb", [128, 512], F32, offset=offs["xt1"])
    tmpb = nc.alloc_sbuf_tensor_at("tmpb", [128, 512], F32, offset=offs["xt1"] + 2048)
    wgp = [nc.alloc_sbuf_tensor_at("wgp%d" % i, [128, KC, 128], BF16, offset=partB0 + 2048 * i) for i in range(16)]
    wg_end = partB0 + 2048 * 16
    PARTB_TOKS = ([("za", 1, tg) for tg in range(NTG)] + [("g", 0), ("g", 1), ("b", 0), ("b", 1), ("b", 2), ("P", 0), ("P", 1), ("P", 2), ("P", 3),
                  ("Pc", 2), ("Pc2", 2), ("w", 0), ("w", 1), ("w", 2), ("wT", 0), ("wT", 1), ("wT", 2), "Pone", ("zsig", 0), ("zsig", 1)])
    cur[0] = arena0
    G0s = [sbp("G0s%d" % i, [128, 512], BF16) for i in range(2)]
    G1s = [sbp("G1s%d" % i, [128, 512], BF16) for i in range(2)]
    t0b = [sbp("t0b%d" % i, [128, 512], F32) for i in range(2)]
    t1b = [sbp("t1b%d" % i, [128, 512], F32) for i in range(2)]
    mT = [sbp("mT%d" % i, [128, KC, 512], BF16) for i in range(2)]
    xr = [sbp("xr%d" % i, [128, D], F32) for i in range(2)]
    obuf = [sbp("obuf%d" % i, [128, D], F32) for i in range(2)]
    s5_end = cur[0]
    assert s5_end <= partA_end, (s5_end, partA_end)
    assert max(attn_end, wg_end) <= top, (attn_end, wg_end, top)
    print("SBUF map: arena0=%d partA_end=%d attn_end=%d wg_end=%d s5_end=%d top=%d" % (arena0, partA_end, attn_end, wg_end, s5_end, top))

    TB = [nc.alloc_psum_tensor("tb%d" % i, [128, 1024], BF16) for i in range(2)]
    ZP = [nc.alloc_psum_tensor("zp%d" % i, [128, 1024], F32) for i in range(2)]
    PB45 = [nc.alloc_psum_tensor("pb%d" % i, [128, 512], F32) for i in (4, 5)]
    PB = [ZP[0][:, 0:512], ZP[0][:, 512:1024], ZP[1][:, 0:512], ZP[1][:, 512:1024], PB45[0], PB45[1]]
    TBF = [TB[0][:, :].bitcast(F32), TB[1][:, :].bitcast(F32)]
    sqs = TB[1]
    sqs5 = TB[1]

    def dma_in(key, out_ap, in_ap, reads=(), writes=(), eng="sp"):
        return P.op(eng, lambda e: e.dma_start(out=out_ap, in_=in_ap), reads=reads, writes=writes, dma_key=key)

    def load_const(dst, src, name):
        dma_in("c_" + name, dst[:], src, writes=[name])

    load_const(identf, ident_d[:, :], "identf")
    load_const(cT, cT_d.rearrange("p (k b) -> p k b", b=NSEQ), "cT")
    load_const(badaT, badaT_d[:, :], "badaT")
    load_const(ngT, ngT_d[:, :], "ngT")
    load_const(bgT, bgT_d[:, :], "bgT")
    load_const(fgT, fgT_d[:, :], "fgT")
    load_const(bfc, bf_d[:, :], "bfc")
    P.op("pool", lambda e: e.tensor_copy(out=ident[:], in_=identf[:]), reads=["identf"], writes=["ident"])
    dma_in("c_mask", mstage[:], maskA_d[:, :], writes=["mstage"])
    P.op("pool", lambda e: e.tensor_copy(out=maskA[:], in_=mstage[:]), reads=["mstage"], writes=["maskA"])
    dma_in("c_mask", mstage[:], maskB_d[:, :], writes=["mstage"])
    P.op("pool", lambda e: e.tensor_copy(out=maskB[:], in_=mstage[:]), reads=["mstage"], writes=["maskB"])
    P.op("pool", lambda e: e.memset(onesf[:], 1.0), writes=["onesf"])
    P.op("pool", lambda e: e.memset(eps_t[:], EPS), writes=["eps"])
    P.op("pool", lambda e: e.memset(ones_c[:], 1.0), writes=["ones_c"])
    P.op("dve", lambda e: e.tensor_scalar(out=negb[:], in0=bfc[:], scalar1=-1.0, scalar2=None, op0=ALU.mult), reads=["bfc"], writes=["negb"])
    P.op("pool", lambda e: e.memset(Pbuf[0][:, 1025:1026], 1.0), writes=["Pone"])
    P.op("pool", lambda e: e.memset(Pbuf[1][:, 1025:1026], 1.0), writes=["Pone"])
    P.op("pool", lambda e: e.memset(Pbuf[2][:, 1025:1026], 1.0), writes=["Pone"])

    stg_i = [0]

    def load_piece(src_ap, kcn, ncols, dst_ap, dst_tokens, cast_eng="pool"):
        i = stg_i[0] % NSTG
        stg_i[0] += 1
        st = stg[i]
        dma_in("stg%d" % i, st[:, 0:kcn, 0:ncols], src_ap, writes=[("stg", i)])
        P.op(cast_eng, lambda e: e.tensor_copy(out=dst_ap, in_=st[:, 0:kcn, 0:ncols]),
             reads=[("stg", i)], writes=dst_tokens)

    def staged_items(pieces, cast_eng):
        slots = {}

        def dma_fn(i):
            src_ap, kcn, ncols, dst_ap, toks = pieces[i]
            k = stg_i[0] % NSTG
            stg_i[0] += 1
            slots[i] = k
            dma_in("stg%d" % k, stg[k][:, 0:kcn, 0:ncols], src_ap, writes=[("stg", k)])

        def cast_fn(i):
            src_ap, kcn, ncols, dst_ap, toks = pieces[i]
            k = slots[i]
            st = stg[k]
            if cast_eng == "act":
                P.op("act", lambda e: e.activation(out=dst_ap, in_=st[:, 0:kcn, 0:ncols], func=AF.Copy),
                     reads=[("stg", k)], writes=toks)
            else:
                P.op(cast_eng, lambda e: e.tensor_copy(out=dst_ap, in_=st[:, 0:kcn, 0:ncols]),
                     reads=[("stg", k)], writes=toks)
        n = len(pieces)
        items = [lambda: [dma_fn(i) for i in range(min(NSTG, n))]]
        for i in range(n):
            def it(i=i):
                cast_fn(i)
                if i + NSTG < n:
                    dma_fn(i + NSTG)
            items.append(it)
        return items

    def chunk_pieces(offq, offk, offv, offz):
        return [(win_v[:, :, off:off + 128], KC, 128, dst[:, :, :], [nm])
                for nm, dst, off in (("wq", wq, offq), ("wk", wk, offk), ("wv", wv, offv), ("wz", wz, offz))]

    win_v = win_d.rearrange("(k p) n -> p k n", p=128)
    wada_v = wada_d.rearrange("(k p) n -> p k n", p=128)
    wosb_v = wosb_d.rearrange("(k p) n -> p k n", p=128)
    wofox_v = wofox_d.rearrange("(k p) n -> p k n", p=128)
    wout_v = wout_d.rearrange("(k p) n -> p k n", p=128)

    def bcast_rows(col_fn, dst, dst_tok, src_tok):
        for half in range(2):
            bank = PB[1]
            for q in range(4):
                kc = half * 4 + q
                dg = diag[kc % 2]
                P.op("dve", lambda e, dg=dg, kc=kc: e.tensor_scalar(out=dg[:], in0=identf[:], scalar1=col_fn(kc),
                                                                     scalar2=None, op0=ALU.mult),
                     reads=["identf", src_tok], writes=[("diag", kc % 2)])
                P.op("pe", lambda e, dg=dg, q=q: e.matmul(bank[:, q * 128:(q + 1) * 128], lhsT=onesf[:], rhs=dg[:],
                                                           start=True, stop=True),
                     reads=["onesf", ("diag", kc % 2)], writes=[("ps", 1)])
            P.op("act", lambda e, half=half: e.activation(out=dst[:, half * 512:(half + 1) * 512], in_=bank[:, :], func=AF.Copy),
                 reads=[("ps", 1)], writes=[dst_tok])

    bcast_rows(lambda kc: fgT[:, kc:kc + 1], fg_bc, "fg_bc", "fgT")

    modps = PB[0]
    P.op("dve", lambda e: e.tensor_copy(out=cTb[:], in_=cT[:]), reads=["cT"], writes=["cTb"])
    def mod_dma(j):
        i = j % 8
        dma_in("stgx%d" % i, stgx[i][:, :, :], wada_v[:, :, j * 128:(j + 1) * 128], writes=[("stgx", i)])

    for j in range(8):
        mod_dma(j)
    for j in range(24):
        i = j % 8
        st = stgx[i]
        wab = xn[j % 2][:, :].rearrange("p (k n) -> p k n", n=128)
        P.op("dve", lambda e, st=st, wab=wab: e.tensor_copy(out=wab, in_=st[:, :, :]), reads=[("stgx", i)], writes=[("xn", j % 2)])

        def mm(e, wab=wab, j=j):
            r = None
            for kc in range(KC):
                r = e.matmul(modps[:, 2 * j:2 * j + 2], lhsT=wab[:, kc, :], rhs=cTb[:, kc, :],
                             start=(kc == 0), stop=(kc == KC - 1))
            return r
        P.op("pe", mm, reads=[("xn", j % 2), "cTb"], writes=[("ps", 0)])
        if j + 8 < 24:
            mod_dma(j + 8)
    modps_v = modps[:, 0:48].rearrange("p (j b) -> p j b", b=NSEQ)
    for b in range(NSEQ):
        P.op("dve", lambda e, b=b: e.tensor_tensor(out=modT[:, :, b], in0=modps_v[:, :, b], in1=badaT[:, :], op=ALU.add),
             reads=[("ps", 0), "badaT"], writes=["modT"])
        P.op("dve", lambda e, b=b: e.scalar_tensor_tensor(out=acoef[:, :, b], in0=modT[:, 8:16, b], scalar=1.0,
                                                           in1=ngT[:, :], op0=ALU.add, op1=ALU.mult),
             reads=["modT", "ngT"], writes=["acoef"])

    def w5_pieces(cbs):
        pcs = []
        for cb in cbs:
            pcs.append((wosb_v[:, :, cb * 128:(cb + 1) * 128], 4, 128, wosb[:, :, cb * 128:(cb + 1) * 128], [("wosb", cb)]))
            pcs.append((wofox_v[:, :, cb * 128:(cb + 1) * 128], 4, 128, wofox[:, :, cb * 128:(cb + 1) * 128], [("wofox", cb)]))
            pcs.append((wout_v[:, :, cb * 128:(cb + 1) * 128], 8, 128, wout[:, :, cb * 128:(cb + 1) * 128], [("wout", cb)]))
        return pcs


    out_i = [0]

    for sq in range(NSEQ):
        row0 = sq * S
        P.op("pool", lambda e: e.memset(VV[:, :, 0, 64:128], 1.0), writes=["VVones"])
        P.op("pool", lambda e: e.memset(VV[:, :, 1, 0:64], 1.0), writes=["VVones"])
        P.op("pool", lambda e: e.memset(Pbuf[0][:, 1025:1026], 1.0), writes=["Pone"])
        P.op("pool", lambda e: e.memset(Pbuf[1][:, 1025:1026], 1.0), writes=["Pone"])
        P.op("pool", lambda e: e.memset(Pbuf[2][:, 1025:1026], 1.0), writes=["Pone"])
        i = stg_i[0] % NSTG
        stg_i[0] += 1
        dma_in("stg%d" % i, stg[i][:, :, 0:8], win_v[:, :, OFF_F:OFF_F + 8], writes=[("stg", i)])
        P.op("pool", lambda e, i=i: e.tensor_copy(out=wfw[:], in_=stg[i][:, :, 0:8]), reads=[("stg", i)], writes=["wfw"])

        def hT_toks(tg):
            r = []
            for tb in range(4 * tg, 4 * tg + 4):
                r += [("hT", tb)]
            return r

        def x_dma(tb):
            xi = tb % 4
            dma_in("xt%d" % xi, xt4[xi][:], x_d[row0 + tb * 128: row0 + (tb + 1) * 128, :], writes=[("xt", xi)])

        def norm0(tb, sq=sq):
            par = tb % 2
            xi = tb % 4
            xb = xt4[xi]
            xnb = xn[par]
            s0 = 3 * par
            xextra = YG_TOKS if xi >= 2 else []
            P.op("act", lambda e: e.activation(out=gbuf[0][:], in_=xb[:], func=AF.Square, accum_out=stat[:, s0:s0 + 1]),
                 reads=[("xt", xi)] + xextra, writes=[("g", 0), ("ss", par)])
            P.op("act", lambda e: e.activation(out=stat[:, s0 + 1:s0 + 2], in_=stat[:, s0:s0 + 1], func=AF.Ln, scale=1.0 / D, bias=eps_t[:, 0:1]),
                 reads=[("ss", par), "eps"], writes=[("sd", par)])
            P.op("act", lambda e: e.activation(out=stat[:, s0 + 2:s0 + 3], in_=stat[:, s0 + 1:s0 + 2], func=AF.Exp, scale=-0.5),
                 reads=[("sd", par)], writes=[("rstd", par)])
            P.op("dve", lambda e: e.tensor_scalar(out=xnb[:], in0=xb[:], scalar1=stat[:, s0 + 2:s0 + 3], scalar2=None, op0=ALU.mult),
                 reads=[("xt", xi), ("rstd", par)] + xextra, writes=[("xn", par)])
            if tb + 4 < NB:
                x_dma(tb + 4)

        def norm1(tb, sq=sq):
            par = tb % 2
            xnb = xn[par]
            tv = TB[par][:, :].rearrange("p (k t) -> p k t", t=128)

            def tr(e):
                r = None
                for kc in range(KC):
                    r = e.transpose(out=tv[:, kc, :], in_=xnb[:, kc * 128:(kc + 1) * 128], identity=ident[:])
                return r
            P.op("pe", tr, reads=[("xn", par), "ident"], writes=[("tb", par)])
            if False:
                def ev(e):
                    r = None
                    for kc in range(KC):
                        r = e.activation(out=hT[:, kc, tb * 128:(tb + 1) * 128], in_=tv[:, kc, :], func=AF.Identity,
                                         scale=acoef[:, kc, sq:sq + 1], bias=modT[:, kc, sq:sq + 1])
                    return r
                P.op("act", ev, reads=[("tb", par), "acoef", "modT"], writes=[("hT", tb)])
            else:
                def ev(e):
                    r = None
                    for kc in range(KC):
                        r = e.tensor_scalar(out=hT[:, kc, tb * 128:(tb + 1) * 128], in0=tv[:, kc, :],
                                            scalar1=acoef[:, kc, sq:sq + 1], scalar2=modT[:, kc, sq:sq + 1],
                                            op0=ALU.mult, op1=ALU.add)
                    return r
                P.op("dve", ev, reads=[("tb", par), "acoef", "modT"], writes=[("hT", tb)])

        def f_phase(tg):
            cols = slice(tg * 512, (tg + 1) * 512)
            fbank = PB[1]

            def fmm(e, cols=cols):
                r = None
                for kc in range(KC):
                    r = e.matmul(fbank[0:8, :], lhsT=wfw[:, kc, :], rhs=hT[:, kc, cols], start=(kc == 0), stop=(kc == KC - 1))
                return r
            P.op("pe", fmm, reads=["wfw"] + hT_toks(tg), writes=[("ps", 1)])
            P.op("act", lambda e: e.activation(out=ftmp[0][:], in_=fbank[0:8, :], func=AF.Exp, bias=negb[:, 0:1], scale=-1.0),
                 reads=[("ps", 1), "negb"] + FT, writes=["f0"])
            P.op("act", lambda e: e.activation(out=ftmp[0][:], in_=ftmp[0][:], func=AF.Ln, bias=ones_c[0:8, 0:1], scale=1.0),
                 reads=["f0", "ones_c"] + FT, writes=["f0"])
            Fc = ftmp[1 + tg % 2]
            Fp = ftmp[1 + (tg + 1) % 2]
            init = 0.0 if tg == 0 else Fp[:, 511:512]
            P.op("dve", lambda e, Fc=Fc, init=init: e.tensor_tensor_scan(out=Fc[:], data0=ftmp[0][:], data1=ftmp[0][:], initial=init,
                                                                          op0=ALU.add, op1=ALU.max),
                 reads=["f0", ("F", (tg + 1) % 2)] + FT, writes=[("F", tg % 2)])
            P.op("dve", lambda e, Fc=Fc, cols=cols: e.tensor_scalar(out=fsp[0:8, cols], in0=Fc[:], scalar1=-8.0, scalar2=None, op0=ALU.mult),
                 reads=[("F", tg % 2)], writes=[("fsp", tg)])
            P.op("dve", lambda e, Fc=Fc, cols=cols: e.scalar_tensor_tensor(out=ftmp[3][:], in0=Fc[:], scalar=-8.0, in1=fsp[0:8, cols],
                                                                            op0=ALU.mult, op1=ALU.subtract),
                 reads=[("F", tg % 2), ("fsp", tg)] + FT, writes=["r1"])
            P.op("dve", lambda e: e.tensor_copy(out=fmid[:], in_=ftmp[3][:]), reads=["r1"] + FT, writes=["fmid"])
            P.op("dve", lambda e, cols=cols: e.tensor_copy(out=fsp[32:40, cols], in_=fmid[:]), reads=["fmid"], writes=[("fsp1", tg)])
            P.op("dve", lambda e, cols=cols: e.tensor_tensor(out=fsp[64:72, cols], in0=ftmp[3][:], in1=fmid[:], op=ALU.subtract),
                 reads=["r1", "fmid"], writes=[("fsp2", tg)])
        fsp_toks = []
        for tg in range(NTG):
            fsp_toks += [("fsp", tg), ("fsp1", tg), ("fsp2", tg)]

        def load_chunk_weights(offs):
            for nm, (dst, off) in offs.items():
                load_piece(win_v[:, :, off:off + 128], KC, 128, dst[:, :, :], [nm])

        def proj_fm(wt, wtok, tg, bank_i):
            bank = PB[bank_i] if bank_i < 10 else TBF[bank_i - 10]
            cols = slice(tg * 512, (tg + 1) * 512)

            def mm(e):
                r = None
                for kc in range(KC):
                    r = e.matmul(bank[:, :], lhsT=wt[:, kc, :], rhs=hT[:, kc, cols], start=(kc == 0), stop=(kc == KC - 1))
                return r
            P.op("pe", mm, reads=[wtok] + hT_toks(tg), writes=[("ps", bank_i) if bank_i < 10 else ("tb", bank_i - 10)])
            return bank

        def proj_v(c):
            for g4 in range(NB // 4):
                bank = TBF[g4 % 2]

                def mm(e, g4=g4, bank=bank):
                    r = None
                    for q in range(4):
                        sbk = g4 * 4 + q
                        for kc in range(KC):
                            r = e.matmul(bank[:, q * 128:(q + 1) * 128], lhsT=hT[:, kc, sbk * 128:(sbk + 1) * 128], rhs=wv[:, kc, :],
                                         start=(kc == 0), stop=(kc == KC - 1))
                    return r
                P.op("pe", mm, reads=["wv"] + hT_toks(g4), writes=[("tb", g4 % 2)])
                bv = bank[:, :].rearrange("p (q n) -> p q n", n=128)

                def ev(e, g4=g4, bv=bv):
                    e.activation(out=VV[:, g4 * 4:(g4 + 1) * 4, 0, 0:64], in_=bv[:, :, 0:64], func=AF.Copy)
                    return e.activation(out=VV[:, g4 * 4:(g4 + 1) * 4, 1, 64:128], in_=bv[:, :, 64:128], func=AF.Copy)
                P.op("act", ev, reads=[("tb", g4 % 2), "VVones"], writes=[("VV", g4)])

        VV_toks = [("VV", g) for g in range(NB // 4)]

        rowc = [0]

        deferred = []

        def flush_deferred():
            while deferred:
                deferred.pop(0)()

        def a_items(c):
            par = c % 2
            qTp = QB[:, par, :]
            kTp = KB[:, par, :]
            zTp = zTa[par]
            Vp = VVa[par]
            items = staged_items(chunk_pieces(OFF_QA + c * 128, OFF_KA + c * 128, OFF_VA + c * 128, OFF_ZA + c * 128), "act" if c == 0 else "dve")

            def mk(wt, wtok, tg, dst, tokname, func):
                def it():
                    cols = slice(tg * 512, (tg + 1) * 512)
                    bk = proj_fm(wt, wtok, tg, 5)

                    def ev():
                        if func is None:
                            zs = zsig[tg % 2]
                            P.op("act", lambda e: e.activation(out=zs[:], in_=bk[:, :], func=AF.Sigmoid),
                                 reads=[("ps", 5)], writes=[("zsig", 0)])
                            P.op("dve", lambda e: e.tensor_tensor(out=dst[:, cols], in0=bk[:, :], in1=zs[:], op=ALU.mult),
                                 reads=[("ps", 5), ("zsig", 0)], writes=[(tokname, par, tg)])
                        else:
                            P.op("act", lambda e: e.activation(out=dst[:, cols], in_=bk[:, :], func=func),
                                 reads=[("ps", 5)], writes=[(tokname, par, tg)])
                    deferred.append(ev)
                return it
            for tg in range(NTG):
                items.append(mk(wq, "wq", tg, qTp, "qa", AF.Copy))
                items.append(mk(wk, "wk", tg, kTp, "ka", AF.Copy))
                items.append(mk(wz, "wz", tg, zTp, "za", None))

            def mkv(g4):
                def it():
                    bank = PB[5]

                    def mm(e):
                        r = None
                        for q in range(4):
                            sbk = g4 * 4 + q
                            for kc in range(KC):
                                r = e.matmul(bank[:, q * 128:(q + 1) * 128], lhsT=hT[:, kc, sbk * 128:(sbk + 1) * 128], rhs=wv[:, kc, :],
                                             start=(kc == 0), stop=(kc == KC - 1))
                        return r
                    P.op("pe", mm, reads=["wv"] + hT_toks(g4), writes=[("ps", 5)])
                    deferred.append(lambda: P.op("act", lambda e: e.activation(
                        out=Vp[:, g4 * 4:(g4 + 1) * 4, :, :], in_=bank[:, :].rearrange("p (q h n) -> p q h n", h=2, n=64), func=AF.Copy),
                        reads=[("ps", 5)], writes=[("va", par, g4), ("xt", par)]))
                return it
            for g4 in range(NB // 4):
                items.append(mkv(g4))
            return items

        items0 = a_items(0)
        for tb in range(4):
            x_dma(tb)
        items0[0]()
        for step in range(NB + 1):
            if step < NB:
                norm0(step)
            if step >= 1:
                norm1(step - 1)
            if 1 <= step <= 4:
                items0[step]()
            if step >= 1 and step % 4 == 0:
                tg = step // 4 - 1
                f_phase(tg)
                for it in items0[5 + 3 * tg: 8 + 3 * tg]:
                    it()
                    flush_deferred()
        for it in items0[17:]:
            it()
            flush_deferred()
        bcast_rows(lambda kc, sq=sq: modT[:, 16 + kc, sq:sq + 1], gate_bc, "gate_bc", "modT")
        def b0_items():
            items = []

            def mkz(tg):
                def it():
                    cols = slice(tg * 512, (tg + 1) * 512)
                    bk = proj_fm(wz, "wz", tg, 5)
                    zs = zsig[tg % 2]

                    def ev():
                        P.op("act", lambda e: e.activation(out=zs[:], in_=bk[:, :], func=AF.Sigmoid),
                             reads=[("ps", 5)], writes=[("zsig", 0)])
                        P.op("dve", lambda e: e.tensor_tensor(out=zT[:, cols], in0=bk[:, :], in1=zs[:], op=ALU.mult),
                             reads=[("ps", 5), ("zsig", 0)], writes=[("z", tg), ("za", 0, tg)])
                    deferred.append(ev)
                return it

            def mkv(g4):
                def it():
                    bank = PB[5]

                    def mm(e):
                        r = None
                        for q in range(4):
                            sbk = g4 * 4 + q
                            for kc in range(KC):
                                r = e.matmul(bank[:, q * 128:(q + 1) * 128], lhsT=hT[:, kc, sbk * 128:(sbk + 1) * 128], rhs=wv[:, kc, :],
                                             start=(kc == 0), stop=(kc == KC - 1))
                        return r
                    P.op("pe", mm, reads=["wv"] + hT_toks(g4), writes=[("ps", 5)])
                    bv = bank[:, :].rearrange("p (q n) -> p q n", n=128)

                    def ev(e):
                        e.activation(out=VV[:, g4 * 4:(g4 + 1) * 4, 0, 0:64], in_=bv[:, :, 0:64], func=AF.Copy)
                        return e.activation(out=VV[:, g4 * 4:(g4 + 1) * 4, 1, 64:128], in_=bv[:, :, 64:128], func=AF.Copy)
                    deferred.append(lambda: P.op("act", ev, reads=[("ps", 5), "VVones"], writes=[("VV", g4)]))
                return it
            for tg in range(NTG):
                items.append(mkz(tg))
            for g4 in range(NB // 4):
                items.append(mkv(g4))
            return items

        def run_chunk_A(c):
            par = c % 2
            qT = QB[:, par, :]
            kT = KB[:, par, :]
            zTc = zTa[par]
            Vc = VVa[par]
            bg = a_items(c + 1) if c < 3 else (staged_items(chunk_pieces(OFF_QB, OFF_KB, OFF_VB, OFF_ZB), "dve") + b0_items())
            if sq == 0:
                bg = bg + staged_items(w5_pieces([2 * c, 2 * c + 1]), "dve")
            k_toks = [("ka", par, tg) for tg in range(NTG)]
            Va_toks = [("va", par, g) for g in range(NB // 4)]
            rows = []
            for tg in range(NTG):
                for hh in range(2):
                    for qb in range(4 * tg, 4 * tg + 4):
                        ncols = (qb + 1) * 128
                        segs = []
                        hi = ncols
                        while hi > 0:
                            lo = max(0, hi - 1024)
                            segs.append((lo, hi))
                            hi = lo
                        pv_i = [0]
                        for si, (lo, hi) in enumerate(segs):
                            rows.append(dict(tg=tg, hh=hh, qb=qb, si=si, lo=lo, hi=hi, segs=segs, pv_i=pv_i,
                                             last=(hh == 1 and qb == 4 * tg + 3 and si == len(segs) - 1)))
            ybank = PB[4]
            p0rot = [0]
            p0of = {}

            def stA0(rw):
                r = rowc[0]
                rowc[0] += 1
                rw["r"] = r
                tg, hh, qb, si, lo, hi = rw["tg"], rw["hh"], rw["qb"], rw["si"], rw["lo"], rw["hi"]
                hb = 64 * hh
                n = hi - lo
                zb = r % 2
                zbanks = (PB[2 * zb], PB[2 * zb + 1])
                gb = gbuf[r % 2]
                ztoks = [("ps", 2 * zb), ("ps", 2 * zb + 1)]

                def qk(e):
                    rr = None
                    for p0 in range(0, n, 512):
                        w_ = min(512, n - p0)
                        bank = zbanks[p0 // 512]
                        isdiag = (si == 0 and p0 + w_ == n)
                        rr = e.matmul(bank[:, 0:w_], lhsT=qT[hb:hb + 64, qb * 128:(qb + 1) * 128],
                                      rhs=kT[hb:hb + 64, lo + p0: lo + p0 + w_], start=True, stop=not isdiag)
                        if isdiag:
                            rr = e.matmul(bank[:, w_ - 128:w_], lhsT=ident[:], rhs=maskA[:], start=False, stop=True)
                    return rr
                P.op("pe", qk, reads=[("qa", par, tg)] + k_toks + ["ident", "maskA"], writes=ztoks)

                def sg(e):
                    return e.activation(out=gb[:, 0:n], in_=ZP[zb][:, 0:n], func=AF.Sigmoid, scale=-0.125)
                P.op("act", sg, reads=ztoks, writes=[("g", r % 2)])
                bb = bbuf[r % 3]

                def sb_(e):
                    return e.activation(out=bb[:, 0:n], in_=ZP[zb][:, 0:n], func=AF.Sigmoid, scale=0.125)
                P.op("act", sb_, reads=ztoks, writes=[("b", r % 3)])

            def stA1a(rw):
                r, si, lo, hi, segs, qb = rw["r"], rw["si"], rw["lo"], rw["hi"], rw["segs"], rw["qb"]
                n = hi - lo
                gb = gbuf[r % 2]
                if si == 0:
                    pidx = p0rot[0] % 3
                    p0rot[0] += 1
                    p0of[qb] = pidx
                    init = 1.0
                    sc_reads = [("g", r % 2), "Pone"]
                else:
                    pprev = p0of[qb]
                    n0 = segs[0][1] - segs[0][0]
                    pidx = 3
                    P.op("dve", lambda e: e.tensor_copy(out=carry[:, 0:1], in_=Pbuf[pprev][:, 1025 - n0:1026 - n0]),
                         reads=[("P", pprev)], writes=[("Pc", 2)])
                    P.op("dve", lambda e: e.tensor_copy(out=Pbuf[3][:, 1025:1026], in_=carry[:, 0:1]),
                         reads=[("Pc", 2)], writes=[("Pc2", 2)])
                    init = carry[:, 0:1]
                    sc_reads = [("g", r % 2), ("Pc", 2)]
                rw["pidx"] = pidx
                Pb_ = Pbuf[pidx]
                P.op("dve", lambda e: e.tensor_tensor_scan(
                    out=Pb_[:, 1025 - n:1025][:, ::-1], data0=gb[:, 0:n][:, ::-1], data1=gb[:, 0:n][:, ::-1],
                    initial=init, op0=ALU.mult, op1=ALU.min),
                    reads=sc_reads, writes=[("P", pidx)])

            def stA1b(rw):
                r, lo, hi, pidx = rw["r"], rw["lo"], rw["hi"], rw["pidx"]
                n = hi - lo
                bb = bbuf[r % 3]
                wb = wbuf[r % 3]
                Pb_ = Pbuf[pidx]
                P.op("dve", lambda e: e.tensor_tensor(
                    out=wb[:, 0:n], in0=Pb_[:, 1026 - n:1026], in1=bb[:, 0:n], op=ALU.mult),
                    reads=[("P", pidx), ("Pc2", 2), "Pone", ("b", r % 3)], writes=[("w", r % 3)])

            def stA2(rw):
                r, lo, hi = rw["r"], rw["lo"], rw["hi"]
                n = hi - lo
                wb = wbuf[r % 3]
                tbk = TB[r % 2]
                wtb = wTb[r % 3]

                def trw(e):
                    rr = None
                    for j in range(n // 128):
                        rr = e.transpose(out=tbk[:, j * 128:(j + 1) * 128], in_=wb[:, j * 128:(j + 1) * 128], identity=ident[:])
                    return rr
                P.op("pe", trw, reads=[("w", r % 3), "ident"], writes=[("tb", r % 2)])
                P.op("act", lambda e: e.activation(out=wtb[:, 0:n], in_=tbk[:, 0:n], func=AF.Copy),
                     reads=[("tb", r % 2)], writes=[("wT", r % 3)])

            def stA3(rw, c=c):
                r, tg, hh, qb, lo, hi, pv_i = rw["r"], rw["tg"], rw["hh"], rw["qb"], rw["lo"], rw["hi"], rw["pv_i"]
                hb = 64 * hh
                n = hi - lo
                nkb = qb + 1
                wtb = wTb[r % 3]

                def pv(e):
                    rr = None
                    for j in range(n // 128):
                        kbk = lo // 128 + j
                        first = (pv_i[0] == 0)
                        pv_i[0] += 1
                        last = (pv_i[0] == nkb)
                        rr = e.matmul(ybank[hb:hb + 64, (qb % 4) * 128:(qb % 4 + 1) * 128],
                                      lhsT=Vc[:, kbk, hh, :], rhs=wtb[:, j * 128:(j + 1) * 128],
                                      start=first, stop=last)
                    return rr
                P.op("pe", pv, reads=[("wT", r % 3)] + Va_toks, writes=[("ps", 4)])
                if rw["last"]:
                    P.op("dve", lambda e: e.tensor_tensor(
                        out=ygA[:, c, tg * 512:(tg + 1) * 512], in0=ybank[:, :], in1=zTc[:, tg * 512:(tg + 1) * 512], op=ALU.mult),
                        reads=[("ps", 4), ("za", par, tg)], writes=[("ygA", tg)])

            return rows, (stA0, stA1a, stA1b, stA2, stA3), bg

        stream = []
        chunk_start = {}
        bgs = {}
        nrows = {}
        for c in range(4):
            rows_c, fns_c, bg_c = run_chunk_A(c)
            chunk_start[len(stream)] = c
            bgs[c] = bg_c
            nrows[c] = len(rows_c)
            stream += [(rw, fns_c) for rw in rows_c]
        nst = len(stream)
        cur_bg = []
        local0 = 0
        nr_c = 1
        for step in range(nst + 6):
            flush_deferred()
            if step < nst:
                if step in chunk_start:
                    while cur_bg:
                        cur_bg.pop(0)()
                        flush_deferred()
                    cur_bg = bgs[chunk_start[step]]
                    nr_c = nrows[chunk_start[step]]
                    local0 = step
                rw, fns = stream[step]
                fns[0](rw)
            for k, d in ((1, 1), (2, 2), (3, 4), (4, 6)):
                if 0 <= step - d < nst:
                    rw, fns = stream[step - d]
                    fns[k](rw)
            ls = step - local0
            if cur_bg and (ls == 0 or (ls >= 3 and (ls % 2 == 1 or 2 * len(cur_bg) > (nr_c + 3 - ls)))):
                cur_bg.pop(0)()
        flush_deferred()
        while cur_bg:
            cur_bg.pop(0)()
            flush_deferred()

        zrot = [0]
        yrot = [0]
        P.op("pool", lambda e: e.memset(QB[64:70, :, :], -1.0),
             writes=["QBaug"] + [("q", tg) for tg in range(NTG)] + [("qa", p_, tg) for p_ in range(2) for tg in range(NTG)])
        P.op("pool", lambda e: e.memset(KB[64:70, :, :], 1.0),
             writes=["KBaug"] + [("k", tg) for tg in range(NTG)] + [("ka", p_, tg) for p_ in range(2) for tg in range(NTG)])
        for c in range(4):
            for hh in range(2):
                for j in range(3):
                    hd = 2 * c + hh
                    dma_in("augq%d%d" % (hh, j), QB[64 + j:65 + j, hh, :], fsp[hd + 32 * j:hd + 32 * j + 1, :], reads=fsp_toks + ["QBaug"],
                           writes=[("qaug", hh, j)], eng="sp")
                    dma_in("augk%d%d" % (hh, j), KB[67 + j:68 + j, hh, :], fsp[hd + 32 * j:hd + 32 * j + 1, :], reads=fsp_toks + ["KBaug"],
                           writes=[("kaug", hh, j)], eng="sp")
            for tg in range(NTG):
                cols = slice(tg * 512, (tg + 1) * 512)
                bk = proj_fm(wq, "wq", tg, 10)

                def evq(e, bk=bk, cols=cols):
                    e.activation(out=QB[0:64, 0, cols], in_=bk[0:64, :], func=AF.Copy)
                    return e.activation(out=QB[0:64, 1, cols], in_=bk[64:128, :], func=AF.Copy)
                P.op("act", evq, reads=[("tb", 0)], writes=[("q", tg)])
                bk = proj_fm(wk, "wk", tg, 11)

                def evk(e, bk=bk, cols=cols):
                    e.activation(out=KB[0:64, 0, cols], in_=bk[0:64, :], func=AF.Copy)
                    return e.activation(out=KB[0:64, 1, cols], in_=bk[64:128, :], func=AF.Copy)
                P.op("act", evk, reads=[("tb", 1)], writes=[("k", tg)])
            for tg in range(NTG if c > 0 else 0):
                cols = slice(tg * 512, (tg + 1) * 512)
                bk = proj_fm(wz, "wz", tg, 10 + tg % 2)
                P.op("act", lambda e, bk=bk, cols=cols: e.activation(out=zT[:, cols], in_=bk[:, :], func=AF.Silu),
                     reads=[("tb", tg % 2)], writes=[("z", tg), ("za", 0, tg)])
            if c > 0:
                proj_v(c)
            pcs = []
            if c < 3:
                pcs += chunk_pieces(OFF_QB + (c + 1) * 128, OFF_KB + (c + 1) * 128, OFF_VB + (c + 1) * 128, OFF_ZB + (c + 1) * 128)
            for cb in range(4 * c, 4 * c + 4):
                pcs.append((win_v[:, :, OFF_G + cb * 128: OFF_G + (cb + 1) * 128], KC, 128, wgp[cb][:, :, :],
                            [("wg", cb)] + PARTB_TOKS))
            bgB = staged_items(pcs, "dve")
            aug_toks = [("qaug", hh, j) for hh in range(2) for j in range(3)] + [("kaug", hh, j) for hh in range(2) for j in range(3)]
            k_toks = [("k", tg) for tg in range(NTG)]
            rows = []
            for tg in range(NTG):
                for hh in range(2):
                    ybi = yrot[0] % 3
                    yrot[0] += 1
                    nkb = 4 * tg + 4
                    for kbk in range(nkb):
                        rows.append(dict(tg=tg, hh=hh, kbk=kbk, nkb=nkb, ybi=ybi))

            def stB0(rw):
                tg, hh, kbk = rw["tg"], rw["hh"], rw["kbk"]
                if kbk < 4 * tg:
                    c0 = 0
                    isdiag = False
                else:
                    c0 = (kbk - 4 * tg) * 128
                    isdiag = True
                rw["c0"] = c0
                zi = zrot[0] % 3
                pi = zrot[0] % 4
                zrot[0] += 1
                rw["pi"] = pi
                zbank = PB[zi]
                pb_ = pbuf[pi]

                def qk(e):
                    rr = e.matmul(zbank[:, c0:512], lhsT=KB[0:70, hh, kbk * 128:(kbk + 1) * 128],
                                  rhs=QB[0:70, hh, tg * 512 + c0:(tg + 1) * 512], start=True, stop=not isdiag)
                    if isdiag:
                        rr = e.matmul(zbank[:, c0:c0 + 128], lhsT=ident[:], rhs=maskB[:], start=False, stop=True)
                    return rr
                P.op("pe", qk, reads=[("q", tg)] + k_toks + aug_toks + ["ident", "maskB", "QBaug", "KBaug"], writes=[("ps", zi)])
                rw["zi"] = zi

            def stB0b(rw):
                c0, zi, pi = rw["c0"], rw["zi"], rw["pi"]
                zbank = PB[zi]
                pb_ = pbuf[pi]
                P.op("act", lambda e: e.activation(out=pb_[:, c0:512], in_=zbank[:, c0:512], func=AF.Exp, scale=0.125),
                     reads=[("ps", zi)], writes=[("p", pi)])

            def stB1(rw, c=c):
                tg, hh, kbk, nkb, ybi, c0, pi = rw["tg"], rw["hh"], rw["kbk"], rw["nkb"], rw["ybi"], rw["c0"], rw["pi"]
                ybank = PB[3 + ybi]
                pb_ = pbuf[pi]
                P.op("pe", lambda e: e.matmul(
                    ybank[:, c0:512], lhsT=VV[:, kbk, hh, :], rhs=pb_[:, c0:512], start=(kbk == 0), stop=(kbk == nkb - 1)),
                    reads=[("p", pi)] + VV_toks + ["VVones"], writes=[("ps", 3 + ybi)])
                if kbk == nkb - 1:
                    ys = slice(0, 64) if hh == 0 else slice(64, 128)
                    ds_ = slice(64, 128) if hh == 0 else slice(0, 64)
                    P.op("dve", lambda e: e.reciprocal(out=recb[ds_, :], in_=ybank[ds_, :]),
                         reads=[("ps", 3 + ybi)], writes=["rec"])
                    P.op("dve", lambda e: e.tensor_tensor(out=tmpb[ys, :], in0=ybank[ys, :], in1=recb[ds_, :], op=ALU.mult),
                         reads=[("ps", 3 + ybi), "rec"], writes=["tmpb"])
                    P.op("dve", lambda e: e.tensor_tensor(out=ygB[ys, c, tg * 512:(tg + 1) * 512], in0=tmpb[ys, :],
                                                          in1=zT[ys, tg * 512:(tg + 1) * 512], op=ALU.mult),
                         reads=["tmpb", ("z", tg)], writes=[("ygB", tg, hh)])

            nr = len(rows)
            for step in range(nr + 3):
                if step < nr:
                    stB0(rows[step])
                if 0 <= step - 1 < nr:
                    stB0b(rows[step - 1])
                if 0 <= step - 3 < nr:
                    stB1(rows[step - 3])
                if bgB and step % 4 == 1:
                    bgB.pop(0)()
            while bgB:
                bgB.pop(0)()

        P.barrier()
        def o_phase(tg, mt):
            mT_toks = [("mT", tg % 2, j) for j in range(8)]
            pending_tail = []
            for qq in range(4):
                tb = tg * 4 + qq
                oi = out_i[0] % 2
                out_i[0] += 1
                xb = xt[oi]
                xrb = xr[oi]
                ob = obuf[oi]
                dma_in("xt%d" % oi, xb[:], x_d[row0 + tb * 128: row0 + (tb + 1) * 128, :], writes=[("xt", oi)])
                for nh in range(2):
                    obank = PB[4 + nh]
                    ncol = slice(nh * 512, (nh + 1) * 512)

                    def mm_o(e, mt=mt, qq=qq, ncol=ncol, obank=obank):
                        rr = None
                        for kc in range(KC):
                            rr = e.matmul(obank[:, :], lhsT=mt[:, kc, qq * 128:(qq + 1) * 128], rhs=wout[:, kc, ncol], start=(kc == 0), stop=(kc == KC - 1))
                        return rr
                    P.op("pe", mm_o, reads=mT_toks + [("wout", cb_) for cb_ in range(8)], writes=[("ps", 4 + nh)])
                    P.op("dve", lambda e, obank=obank, ncol=ncol, xrb=xrb: e.tensor_tensor(out=xrb[:, ncol], in0=obank[:, :], in1=gate_bc[:, ncol], op=ALU.mult),
                         reads=[("ps", 4 + nh), "gate_bc"], writes=[("xr", oi, nh), ("xr2", oi, nh)])
                s5 = 3 * oi
                for nh in range(2):
                    ncol = slice(nh * 512, (nh + 1) * 512)
                    P.op("dve", lambda e, ncol=ncol, xrb=xrb, xb=xb: e.tensor_tensor(out=xrb[:, ncol], in0=xrb[:, ncol], in1=xb[:, ncol], op=ALU.add),
                         reads=[("xr", oi, nh), ("xt", oi)], writes=[("xr2", oi, nh)])
                P.op("act", lambda e, xrb=xrb, ob=ob, s5=s5: e.activation(out=ob[:], in_=xrb[:], func=AF.Square, accum_out=stat[:, s5:s5 + 1]),
                     reads=[("xr2", oi, 0), ("xr2", oi, 1)], writes=[("ob", oi), ("ss5", oi)])
                P.op("act", lambda e, s5=s5: e.activation(out=stat[:, s5 + 1:s5 + 2], in_=stat[:, s5:s5 + 1], func=AF.Ln, scale=1.0 / D, bias=eps_t[:, 0:1]),
                     reads=[("ss5", oi), "eps"], writes=[("sd5", oi)])
                P.op("act", lambda e, s5=s5: e.activation(out=stat[:, s5 + 2:s5 + 3], in_=stat[:, s5 + 1:s5 + 2], func=AF.Exp, scale=-0.5),
                     reads=[("sd5", oi)], writes=[("rstd5", oi)])

                while pending_tail:
                    pending_tail.pop(0)()

                def tail(oi=oi, xrb=xrb, ob=ob, tb=tb, s5=s5):
                    P.op("dve", lambda e: e.scalar_tensor_tensor(out=ob[:], in0=xrb[:], scalar=stat[:, s5 + 2:s5 + 3], in1=fg_bc[:],
                                                                 op0=ALU.mult, op1=ALU.mult),
                         reads=[("xr2", oi, 0), ("xr2", oi, 1), ("rstd5", oi), "fg_bc"], writes=[("ob", oi)])
                    dma_in("out%d" % oi, out_d[row0 + tb * 128: row0 + (tb + 1) * 128, :], ob[:], reads=[("ob", oi)], writes=[("outd", tb, sq)], eng="pool")
                pending_tail.append(tail)
            while pending_tail:
                pending_tail.pop(0)()
        for tg in range(NTG):
            cols = slice(tg * 512, (tg + 1) * 512)
            mt = mT[tg % 2]
            for j in range(8):
                jb = (tg * 8 + j) % 2
                fcols = slice(j * 128, (j + 1) * 128)

                def mm_pa(e, fcols=fcols, cols=cols):
                    rr = None
                    for kc in range(4):
                        rr = e.matmul(PB[0][:, :], lhsT=wosb[:, kc, fcols], rhs=ygA[:, kc, cols], start=(kc == 0), stop=(kc == 3))
                    return rr

                def mm_pb(e, fcols=fcols, cols=cols):
                    rr = None
                    for kc in range(4):
                        rr = e.matmul(PB[1][:, :], lhsT=wofox[:, kc, fcols], rhs=ygB[:, kc, cols], start=(kc == 0), stop=(kc == 3))
                    return rr

                gbank = (PB[2], PB[3]) if j % 2 == 0 else (TBF[0], TBF[1])
                gtok = (("ps", 2), ("ps", 3)) if j % 2 == 0 else (("tb", 0), ("tb", 1))

                def mm_g(e, j=j, cols=cols, which=0, gbank=gbank):
                    rr = None
                    for kc in range(KC):
                        rr = e.matmul(gbank[which][:, :], lhsT=wgp[which * 8 + j][:, kc, :], rhs=hT[:, kc, cols],
                                      start=(kc == 0), stop=(kc == KC - 1))
                    return rr
                P.op("pe", lambda e, f=mm_g: f(e, which=0), reads=[("wg", j)] + hT_toks(tg), writes=[gtok[0]])
                P.op("act", lambda e, j=j, jb=jb, gbank=gbank: e.activation(out=G0s[jb][:], in_=gbank[0][:, :], func=AF.Sigmoid, bias=bgT[:, j:j + 1], scale=1.0),
                     reads=[gtok[0], "bgT"], writes=[("G0", jb)])
                P.op("pe", lambda e, f=mm_g: f(e, which=1), reads=[("wg", 8 + j)] + hT_toks(tg), writes=[gtok[1]])
                P.op("act", lambda e, j=j, jb=jb, gbank=gbank: e.activation(out=G1s[jb][:], in_=gbank[1][:, :], func=AF.Sigmoid, bias=bgT[:, 8 + j:9 + j], scale=1.0),
                     reads=[gtok[1], "bgT"], writes=[("G1", jb)])
                P.op("pe", mm_pa, reads=[("wosb", j), ("ygA", tg)], writes=[("ps", 0)])
                P.op("pe", mm_pb, reads=[("wofox", j), ("ygB", tg, 0), ("ygB", tg, 1)], writes=[("ps", 1)])
                P.op("dve", lambda e, jb=jb: e.tensor_tensor(out=t0b[jb][:], in0=PB[0][:, :], in1=G0s[jb][:], op=ALU.mult),
                     reads=[("ps", 0), ("G0", jb)], writes=[("t0", jb)])
                P.op("dve", lambda e, jb=jb: e.tensor_tensor(out=t1b[jb][:], in0=PB[1][:, :], in1=G1s[jb][:], op=ALU.mult),
                     reads=[("ps", 1), ("G1", jb)], writes=[("t1", jb)])
                P.op("pool", lambda e, jb=jb, mt=mt, j=j: e.tensor_tensor(out=mt[:, j, :], in0=t0b[jb][:], in1=t1b[jb][:], op=ALU.add),
                     reads=[("t0", jb), ("t1", jb)], writes=[("mT", tg % 2, j)])
                if j == 1 and tg >= 1:
                    o_phase(tg - 1, mT[(tg - 1) % 2])
        o_phase(NTG - 1, mT[(NTG - 1) % 2])
        P.barrier()

    P.emit()
    return nc


def _build():
    return build_nc()


_NC_CACHE = {}


def _host_layout(inputs, core):
    b0 = core * NSEQ
    f32 = np.float32
    x = np.ascontiguousarray(inputs["x"][b0:b0 + NSEQ].reshape(NSEQ * S, D), dtype=f32)
    c = np.asarray(inputs["c"][b0:b0 + NSEQ], dtype=f32)
    cT = np.ascontiguousarray(c.reshape(NSEQ, KC, 128).transpose(2, 1, 0).reshape(128, KC * NSEQ))

    def fm(v, n):
        return np.ascontiguousarray(np.asarray(v, dtype=f32).reshape(n, 128).T)

    t = np.arange(128)
    maskA = np.where(t[None, :] >= t[:, None], MASKVAL, 0.0).astype(f32)
    maskB = np.where(t[:, None] > t[None, :], MASKVAL, 0.0).astype(f32)
    return {
        "x": x, "cT": cT,
        "w_ada": np.ascontiguousarray(inputs["w_ada"][0], dtype=f32),
        "badaT": fm(inputs["b_ada"][0], 24),
        "ngT": fm(inputs["norm_g"][0], KC),
        "w_in": np.ascontiguousarray(inputs["w_in"][0], dtype=f32),
        "bf": np.ascontiguousarray(np.asarray(inputs["b_forget"][0], dtype=f32).reshape(8, 1)),
        "w_o_sb": np.ascontiguousarray(inputs["w_o_sb"][0], dtype=f32),
        "w_o_fox": np.ascontiguousarray(inputs["w_o_fox"][0], dtype=f32),
        "bgT": fm(inputs["b_gate"][0], 16),
        "w_out": np.ascontiguousarray(inputs["w_out"][0], dtype=f32),
        "fgT": fm(inputs["final_g"], KC),
        "identf": np.eye(128, dtype=f32),
        "maskA": maskA, "maskB": maskB,
    }


def kernel(**inputs):
    inputs = {k: np.asarray(v) for k, v in inputs.items()}
    if "nc" not in _NC_CACHE:
        _NC_CACHE["nc"] = _build()
    nc = _NC_CACHE["nc"]
    in_maps = [_host_layout(inputs, i) for i in range(NCORES)]
    res = run_bass_kernel_spmd(nc, in_maps, core_ids=list(range(NCORES)))
    outs = [np.asarray(r["out"]).reshape(NSEQ, S, D) for r in res.results]
    return np.concatenate(outs, axis=0).astype(np.float32)
```
